# Optimizing a Trainium2 kernel written in Bass

```python
import math
import jax, jax.numpy as jnp
from jax import lax
import numpy as np


D_MODEL = 2048
BATCH = 2
SEQ = 16384
DEPTH = 1

D_MIX = D_MODEL
D_CONV = D_MIX // 2
CONV_GROUPS = 8
CONV_K = 31
D_MLSTM = D_MIX - D_CONV
MLSTM_HEADS = 4
MLSTM_HD = D_MLSTM // MLSTM_HEADS
QK_CONV_K = 4
CHUNK = 64
D_FF = 5632
FFN_CONV_K = 3
EPS = 1e-6
M_INIT = -1e30

OFF_CA = 0
OFF_CG = OFF_CA + D_CONV
OFF_Q = OFF_CG + D_CONV
OFF_K = OFF_Q + D_MLSTM
OFF_V = OFF_K + D_MLSTM
OFF_O = OFF_V + D_MLSTM
OFF_I = OFF_O + D_MLSTM
OFF_F = OFF_I + MLSTM_HEADS
D_IN = OFF_F + MLSTM_HEADS

kernel_name = "hymba_conformerconv_mlstm_convffn"


def rmsnorm(x, g):
    xf = x.astype(jnp.float32)
    y = xf * lax.rsqrt(jnp.mean(xf * xf, axis=-1, keepdims=True) + EPS)
    return (y * g.astype(jnp.float32)).astype(x.dtype)


def group_norm(x, n_groups, gain, bias=None):
    shp = x.shape
    xf = x.astype(jnp.float32).reshape(shp[:-1] + (n_groups, shp[-1] // n_groups))
    mu = jnp.mean(xf, axis=-1, keepdims=True)
    var = jnp.mean(jnp.square(xf - mu), axis=-1, keepdims=True)
    y = ((xf - mu) * lax.rsqrt(var + EPS)).reshape(shp) * gain.astype(jnp.float32)
    if bias is not None:
        y = y + bias.astype(jnp.float32)
    return y.astype(x.dtype)


def causal_dwconv(x, w, b):
    K = w.shape[0]
    y = lax.conv_general_dilated(
        x, w[:, None, :].astype(x.dtype), window_strides=(1,), padding=[(K - 1, 0)],
        dimension_numbers=("NWC", "WIO", "NWC"), feature_group_count=x.shape[-1])
    return y + b.astype(x.dtype)


def mlstm_chunkwise(q, k, v, li, lf):
    B, S, H, D = q.shape
    NC = S // CHUNK

    def to_chunks(t):
        return t.reshape(B, NC, CHUNK, H, t.shape[-1]).transpose(1, 0, 3, 2, 4)

    qc, kc, vc = to_chunks(q), to_chunks(k), to_chunks(v)
    ic = to_chunks(li[..., None])[..., 0]
    fc = to_chunks(lf[..., None])[..., 0]
    causal = jnp.tril(jnp.ones((CHUNK, CHUNK), dtype=bool))

    def step(carry, xs):
        C, n, m = carry
        qj, kj, vj, ij, fj = xs
        b = jnp.cumsum(fj, axis=-1)
        Dlog = b[..., :, None] - b[..., None, :] + ij[..., None, :]
        Dlog = jnp.where(causal, Dlog, -jnp.inf)
        a = b + m[..., None]
        m_t = jnp.maximum(a, jnp.max(Dlog, axis=-1))
        w_intra = jnp.exp(Dlog - m_t[..., None])
        w_inter = jnp.exp(a - m_t)
        s = jnp.einsum("bhld,bhsd->bhls", qj, kj) * w_intra
        num = (w_inter[..., None] * jnp.einsum("bhld,bhde->bhle", qj, C)
               + jnp.einsum("bhls,bhse->bhle", s, vj))
        den = w_inter * jnp.einsum("bhld,bhd->bhl", qj, n) + jnp.sum(s, axis=-1)
        h = num / jnp.maximum(jnp.abs(den), jnp.exp(-m_t))[..., None]
        b_end = b[..., -1]
        g = b_end[..., None] - b + ij
        m_new = jnp.maximum(b_end + m, jnp.max(g, axis=-1))
        decay = jnp.exp(b_end + m - m_new)
        w_in = jnp.exp(g - m_new[..., None])
        C_new = decay[..., None, None] * C + jnp.einsum("bhs,bhsd,bhse->bhde", w_in, kj, vj)
        n_new = decay[..., None] * n + jnp.einsum("bhs,bhsd->bhd", w_in, kj)
        return (C_new, n_new, m_new), h

    init = (jnp.zeros((B, H, D, D), jnp.float32),
            jnp.zeros((B, H, D), jnp.float32),
            jnp.full((B, H), M_INIT, jnp.float32))
    _, hc = lax.scan(step, init, (qc, kc, vc, ic, fc))
    return hc.transpose(1, 0, 3, 2, 4).reshape(B, S, H, D)


def setup_inputs(seed: int = 0) -> dict:
    key = jax.random.key(seed)
    ks = jax.random.split(key, 20)
    f32 = jnp.float32

    def nrm(k, shape, scale):
        return jax.random.normal(k, shape, f32) * scale

    x = nrm(ks[0], (BATCH, SEQ, D_MODEL), 1.0)
    norm_mix_g = 1.0 + nrm(ks[1], (D_MODEL,), 0.02)
    w_in = nrm(ks[2], (D_MODEL, D_IN), D_MODEL ** -0.5)
    b_in = nrm(ks[3], (D_IN,), 0.02)
    b_in = b_in.at[OFF_F:OFF_F + MLSTM_HEADS].add(3.0)
    conv_dw_w = nrm(ks[4], (CONV_K, D_CONV), CONV_K ** -0.5)
    conv_dw_b = nrm(ks[5], (D_CONV,), 0.02)
    conv_gn_g = 1.0 + nrm(ks[6], (D_CONV,), 0.02)
    conv_gn_b = nrm(ks[7], (D_CONV,), 0.02)
    qk_conv_w = nrm(ks[8], (QK_CONV_K, 2 * D_MLSTM), QK_CONV_K ** -0.5)
    qk_conv_b = nrm(ks[9], (2 * D_MLSTM,), 0.02)
    mlstm_hn_g = 1.0 + nrm(ks[10], (D_MLSTM,), 0.02)
    w_out = nrm(ks[11], (D_MIX, D_MODEL), D_MIX ** -0.5)
    norm_ffn_g = 1.0 + nrm(ks[12], (D_MODEL,), 0.02)
    w_up = nrm(ks[13], (D_MODEL, 2 * D_FF), D_MODEL ** -0.5)
    ffn_conv_w = nrm(ks[14], (FFN_CONV_K, D_FF), FFN_CONV_K ** -0.5)
    ffn_conv_b = nrm(ks[15], (D_FF,), 0.02)
    w_down = nrm(ks[16], (D_FF, D_MODEL), D_FF ** -0.5)
    norm_final_g = 1.0 + nrm(ks[17], (D_MODEL,), 0.02)
    return {"x": x, "norm_mix_g": norm_mix_g, "w_in": w_in, "b_in": b_in,
            "conv_dw_w": conv_dw_w, "conv_dw_b": conv_dw_b,
            "conv_gn_g": conv_gn_g, "conv_gn_b": conv_gn_b,
            "qk_conv_w": qk_conv_w, "qk_conv_b": qk_conv_b,
            "mlstm_hn_g": mlstm_hn_g, "w_out": w_out,
            "norm_ffn_g": norm_ffn_g, "w_up": w_up,
            "ffn_conv_w": ffn_conv_w, "ffn_conv_b": ffn_conv_b,
            "w_down": w_down, "norm_final_g": norm_final_g}


def reference(x, norm_mix_g, w_in, b_in, conv_dw_w, conv_dw_b, conv_gn_g, conv_gn_b,
              qk_conv_w, qk_conv_b, mlstm_hn_g, w_out, norm_ffn_g, w_up,
              ffn_conv_w, ffn_conv_b, w_down, norm_final_g):
    B, S, _ = x.shape
    for _layer in range(DEPTH):
        hn = rmsnorm(x, norm_mix_g)
        p = hn @ w_in + b_in

        u = p[..., OFF_CA:OFF_CG] * jax.nn.sigmoid(p[..., OFF_CG:OFF_Q])
        u = causal_dwconv(u, conv_dw_w, conv_dw_b)
        u = jax.nn.silu(group_norm(u, CONV_GROUPS, conv_gn_g, conv_gn_b))

        qk = jax.nn.silu(causal_dwconv(p[..., OFF_Q:OFF_V], qk_conv_w, qk_conv_b))
        q = qk[..., :D_MLSTM].astype(jnp.float32).reshape(B, S, MLSTM_HEADS, MLSTM_HD)
        k = (qk[..., D_MLSTM:].astype(jnp.float32) * (MLSTM_HD ** -0.5)).reshape(B, S, MLSTM_HEADS, MLSTM_HD)
        v = p[..., OFF_V:OFF_O].astype(jnp.float32).reshape(B, S, MLSTM_HEADS, MLSTM_HD)
        o = jax.nn.sigmoid(p[..., OFF_O:OFF_I])
        li = p[..., OFF_I:OFF_F].astype(jnp.float32)
        lf = jax.nn.log_sigmoid(p[..., OFF_F:D_IN].astype(jnp.float32))
        hm = mlstm_chunkwise(q, k, v, li, lf).reshape(B, S, D_MLSTM)
        hm = group_norm(hm, MLSTM_HEADS, mlstm_hn_g).astype(x.dtype) * o

        mix = jnp.concatenate([u.astype(x.dtype), hm], axis=-1) @ w_out
        x = x + mix

        hf = rmsnorm(x, norm_ffn_g)
        up = hf @ w_up
        gate = causal_dwconv(up[..., :D_FF], ffn_conv_w, ffn_conv_b)
        x = x + (jax.nn.silu(gate) * up[..., D_FF:]) @ w_down
    return rmsnorm(x, norm_final_g)
```

```python
import numpy as np
from contextlib import ExitStack
import concourse.bass as bass
import concourse.mybir as mybir
from concourse.bass_utils import run_bass_kernel_spmd

F32 = mybir.dt.float32
BF16 = mybir.dt.bfloat16
AF = mybir.ActivationFunctionType
ALU = mybir.AluOpType
AX = mybir.AxisListType

D = 2048
KC = 16
DIN = 6152
DFF = 5632
NFC = 44
NEG = -1.0e4
EPS = 1e-6
LN16 = float(np.log(16.0))

G1, G2, BA, BG, BQK, CW, CB, GNG, GNB, QKW, QKB, FW, FB, HG = 0, 16, 32, 40, 48, 64, 312, 320, 328, 336, 400, 416, 548, 592
NCV = 600
NBLK = 54
BLK_ORDER = [6, 7, 8, 9, 4, 5, 10, 11, 0, 2, 1, 3, 12, 13, 14, 15]
for _b in range(11):
    BLK_ORDER += [16 + _b, 27 + _b]
BLK_ORDER += list(range(38, 54))


class Res:
    __slots__ = ("name", "w", "r", "alias")

    def __init__(self, name):
        self.name = name
        self.w = None
        self.r = {}
        self.alias = ()


class Buf:
    def __init__(self, t, name):
        self.t = t
        self.res = Res(name)


class RR:
    def __init__(self, bufs):
        self.bufs = bufs
        self.i = 0
        self.pinned = set()

    def get(self, pin=False):
        while True:
            b = self.bufs[self.i % len(self.bufs)]
            self.i += 1
            if id(b) not in self.pinned:
                break
        if pin:
            self.pinned.add(id(b))
        return b

    def unpin(self, b):
        self.pinned.discard(id(b))


class Sched:
    ENG = ("pe", "act", "dve", "pool", "sp")

    def __init__(self):
        self.plan = {e: [] for e in self.ENG}
        self.cnt = {}
        self.seen = {e: {} for e in self.ENG}
        self.last_dma = {}
        self.dry = False

    def op(self, eng, fn, reads=(), writes=(), dma=None, ndma=1, unit=16):
        if self.dry:
            return
        deps = {}

        def add(tok, raw):
            if tok is None:
                return
            k, v = tok
            if k == eng and (not raw or eng == "pe"):
                return
            if deps.get(k, 0) < v:
                deps[k] = v

        for r in reads:
            add(r.w, True)
        for w in writes:
            add(w.w, False)
            for k, v in w.r.items():
                add((k, v), False)
            for a in w.alias:
                add(a.w, False)
                for k, v in a.r.items():
                    add((k, v), False)
        if dma is not None:
            add(self.last_dma.get(dma), True)
        waits = []
        for k, v in deps.items():
            if self.seen[eng].get(k, 0) >= v:
                continue
            self.seen[eng][k] = v
            waits.append((k, v))
        if dma is not None:
            k, inc = dma, unit * ndma
        else:
            k, inc = eng, 1
        self.cnt[k] = self.cnt.get(k, 0) + inc
        tok = (k, self.cnt[k])
        if dma is not None:
            self.last_dma[dma] = tok
        self.plan[eng].append((waits, fn, k, unit if dma is not None else 0))
        for r in reads:
            if r.r.get(k, 0) < tok[1]:
                r.r[k] = tok[1]
        for w in writes:
            w.w = tok
            w.r = {}

    def replay(self, eng, e, sems):
        for waits, fn, k, unit in self.plan[eng]:
            for wk, wv in waits:
                e.wait_ge(sems[wk], wv)
            r = fn(e)
            if unit:
                for ins in r:
                    ins.then_inc(sems[k], unit)
            elif r is not None:
                r.then_inc(sems[k], 1)


class WStream:
    def __init__(self):
        self.order = []
        self.pos = 0
        self.issued = 0
        self.slots = None
        self.loaded = {}

    def get(self, S, blk, issue):
        if S.dry:
            self.order.append(blk)
            return self.slots[0]
        assert self.order[self.pos] == blk, (self.pos, self.order[self.pos], blk)
        while self.issued < min(len(self.order), self.pos + 2):
            i = self.issued
            slot = self.slots[i % len(self.slots)]
            issue(self.order[i], slot)
            self.loaded[i] = slot
            self.issued += 1
        slot = self.loaded.pop(self.pos)
        self.pos += 1
        return slot


def build(NOWN, dbg=None):
    dbg = dbg or []
    nc = bass.Bass("TRN2", target_bir_lowering=False)
    NSUBT = 3 + 4 * NOWN
    ROWS = 128 * NSUBT

    def din(name, shape, dt=F32):
        return nc.dram_tensor(name, shape, dt, kind="ExternalInput").ap()

    xh = din("xh", [ROWS, D])
    wsh = din("wsh", [7, 128, 8192])
    cvd = din("cv", [128, NCV])
    g3d = din("g3bc", [128, D])
    bvod = din("bvo", [1, 2048])
    gbd = din("gbias", [4, 2])
    wgd = din("wgate", [D, 8])
    identd = din("ident", [128, 128])
    maskTd = din("maskT", [128, 128])
    rmaskd = din("rmask", [4, 512])
    eye4d = din("eye4", [4, 4])
    tokmaskd = din("tokmask", [128, 640])
    negmd = din("negm", [4, 640])
    cfgd = din("cfg", [4, 80])
    outd = nc.dram_tensor("out", [512 * NOWN, D], F32, kind="ExternalOutput").ap()
    dbgd = {}
    for name, shape in dbg:
        dbgd[name] = nc.dram_tensor("dbg_" + name, list(shape), F32, kind="ExternalOutput").ap()
    cgin = [nc.dram_tensor(f"cgin{g}", [128, 8192], BF16) for g in range(7)]
    cgout = [nc.dram_tensor(f"cgout{g}", [8 * 128, 8192], BF16) for g in range(7)]
    ccin = nc.dram_tensor("ccin", [129, 2056], F32)
    ccout = nc.dram_tensor("ccout", [8 * 129, 2056], F32)

    es = ExitStack()
    with es:
        def sb(name, shape, dt=F32):
            return es.enter_context(nc.sbuf_tensor("sb_" + name, shape, dt))

        ws_t = [sb(f"ws{i}", [128, 8192], BF16) for i in range(2)]
        xm_t = [sb(f"xm{i}", [128, D]) for i in range(4)]
        hT = sb("hT", [128, KC, 512], BF16)
        hn_tm = sb("hn_tm", [128, D], BF16)
        Cst = sb("Cst", [128, 4, 2, 257])
        cv = sb("cv", [128, NCV])
        cvx = sb("cvx", [128, 16])
        g3bc = sb("g3bc", [128, D])
        ident_f = sb("ident_f", [128, 128])
        ident_b = sb("ident_b", [128, 128], BF16)
        maskT = sb("maskT", [128, 128])
        onesm = sb("onesm", [128, 128])
        ones4 = sb("ones4", [4, 128])
        ones2b = sb("ones2b", [2, 128], BF16)
        eye4 = sb("eye4", [4, 4])
        rmask = sb("rmask", [4, 512])
        neghalf = sb("neghalf", [128, 512])
        wg = sb("wg", [128, KC, 8], BF16)
        bias_vo = sb("bias_vo", [2, 2048], BF16)
        gb = sb("gb", [4, 4])
        qkhalo = sb("qkhalo", [128, 16, 3])
        khalo0 = sb("khalo0", [128, 8, 3])
        uhalo = sb("uhalo", [128, 8, 30])
        ghalo = sb("ghalo", [128, NFC, 2])
        mcur = sb("mcur", [4, 1])
        bsum = sb("bsum", [4, 1])
        stat = sb("stat", [128, 128])
        gsm = sb("gsm", [4, 64])
        cfg = sb("cfg", [4, 80])
        ovl = sb("ovl", [128, 12352])
        qkraw_t = [sb(f"qkraw{i}", [128, 515]) for i in range(2)]
        upre_t = [sb(f"upre{i}", [128, 542]) for i in range(2)]
        tmp_t = [sb(f"tmp{i}", [128, 512]) for i in range(8)]
        ktil_t = [sb(f"ktil{i}", [128, 1024], BF16) for i in range(2)]
        St_t = [sb(f"St{i}", [128, 128], BF16) for i in range(2)]
        Cbf_t = [sb(f"Cbf{i}", [128, 2, 257], BF16) for i in range(2)]
        hh_t = [sb(f"hh{i}", [128, 256]) for i in range(2)]
        hmt_t = [sb(f"hmt{i}", [128, 1024], BF16) for i in range(2)]
        ut = sb("ut", [128, 4, 8])
        dmb = sb("dmb", [128, 4, 8])
        pst = [es.enter_context(nc.psum_tensor(f"ps{i}", [128, 512], F32)) for i in range(8)]

        ob = ovl[:].bitcast(BF16)
        act_v = ob[:, 0:NFC * 512].rearrange("p (c t) -> p c t", c=NFC)
        mixT_v = ob[:, 0:8192].rearrange("p (c t) -> p c t", c=16)
        qT_v = ob[:, 8192:12288].rearrange("p (c t) -> p c t", c=8)
        kT_v = ob[:, 12288:16384].rearrange("p (c t) -> p c t", c=8)
        vext_v = ob[:, 16384:16384 + 4112].rearrange("p (j h e) -> p j h e", j=4, h=4)
        oth_v = ob[:, 20512:20512 + 4096].rearrange("p (j n) -> p j n", j=4)
        brow = ovl[0:1, 0:8192].rearrange("p (r n) -> p r n", r=4)
        browb = ovl[0:1, 8192:10240].bitcast(BF16).rearrange("p (r n) -> p r n", r=2)
        cstage_v = [ovl[:, 0:2056], ovl[:, 2056:4112]]

        sem_names = list(Sched.ENG) + [f"W{i}" for i in range(8)] + ["ws0", "ws1", "x0", "x1", "x2", "x3",
                                                                      "misc", "cc", "cs0", "cs1", "dbg", "GX"] + [f"G{i}" for i in range(7)]
        sems = {n: es.enter_context(nc.semaphore("s_" + n)) for n in sem_names}

        WS = WStream()

        def emit(S):
            ws = [Buf(ws_t[i], f"ws{i}") for i in range(2)]
            WS.slots = ws
            xm = [Buf(xm_t[i], f"xm{i}") for i in range(4)]
            PS = RR([Buf(pst[i], f"ps{i}") for i in range(8)])
            TMP = RR([Buf(tmp_t[i], f"tmp{i}") for i in range(8)])
            QKR = RR([Buf(qkraw_t[i], f"qkraw{i}") for i in range(2)])
            UPR = RR([Buf(upre_t[i], f"upre{i}") for i in range(2)])
            KTL = RR([Buf(ktil_t[i], f"ktil{i}") for i in range(2)])
            STL = RR([Buf(St_t[i], f"St{i}") for i in range(2)])
            CBF = RR([Buf(Cbf_t[i], f"Cbf{i}") for i in range(2)])
            HH = RR([Buf(hh_t[i], f"hh{i}") for i in range(2)])
            HMT = RR([Buf(hmt_t[i], f"hmt{i}") for i in range(2)])
            r_hT = [Res(f"hT{j}") for j in range(4)]
            r_hn = Res("hn_tm")
            r_C = [Res(f"C{h}") for h in range(4)]
            r_const = Res("const")
            r_qkh, r_kh0, r_uh, r_gh = Res("qkhalo"), Res("khalo0"), Res("uhalo"), Res("ghalo")
            r_mcur, r_bsum, r_gsm, r_cfg = Res("mcur"), Res("bsum"), Res("gsm"), Res("cfg")
            r_ut, r_dmb = Res("ut"), Res("dmb")
            r_act, r_mix, r_qT, r_kT, r_vext, r_oth = (Res("act"), Res("mixT"), Res("qT"), Res("kT"),
                                                       Res("vext"), Res("oth"))
            grp = [r_act, r_mix, r_qT, r_kT, r_vext, r_oth]
            r_act.alias = tuple(grp[1:])
            for r in grp[1:]:
                r.alias = (r_act,)
            r_out = []
            r_cc = Res("cc")
            stat_i = [0]
            r_statg = [Res(f"stat{g}") for g in range(16)]

            def stat_col(n=1):
                g = stat_i[0] % 16
                stat_i[0] += 1
                return g * 8

            def act_op(out, in_, func, reads, writes, bias=None, scale=None, accum=None):
                kw = {}
                if bias is not None:
                    kw["bias"] = bias
                if scale is not None:
                    kw["scale"] = scale
                if accum is not None:
                    kw["accum_out"] = accum
                S.op("act", lambda e: e.activation(out=out, in_=in_, func=func, **kw), reads, writes)

            def dve(fn, reads, writes):
                S.op("dve", fn, reads, writes)

            def mm(out, pairs, reads, writes, first=True, last=True):
                def fn(e):
                    ins = None
                    n = len(pairs)
                    for i, (l, r) in enumerate(pairs):
                        ins = e.matmul(out, l, r, start=(first and i == 0), stop=(last and i == n - 1))
                    return ins
                S.op("pe", fn, reads, writes)

            def dma(eng, out, in_, reads, writes, sem, nc_ok=False):
                if nc_ok:
                    S.op(eng, lambda e: [e.dma_start(out=out, in_=in_, allow_slow_non_contiguous=True)],
                         reads, writes, dma=sem)
                else:
                    S.op(eng, lambda e: [e.dma_start(out=out, in_=in_)], reads, writes, dma=sem)

            def debug_dump(name, ap, res):
                if name in dbgd and not S.dry:
                    rl = list(res) if isinstance(res, (list, tuple)) else [res]
                    dma("pool", dbgd[name], ap, rl, [Res("dbgout")], "dbg")

            r_gath = [Res(f"gath{g}") for g in range(7)]
            r_cgin = [Res(f"cgin{g}") for g in range(7)]

            def issue_wload(blk, slot):
                k = ws.index(slot)
                g, r = divmod(BLK_ORDER.index(blk), 8)
                dma("sp", slot.t[:], cgout[g][r * 128:(r + 1) * 128, :], [r_gath[g]], [slot.res], f"ws{k}")

            def wget(blk):
                return WS.get(S, blk, issue_wload)

            def wview(slot, nk=16):
                return slot.t[:, 0:nk * 512].rearrange("p (kc n) -> p kc n", kc=nk)

            dma("sp", cv[:], cvd, [], [r_const], "misc")
            dma("sp", ident_f[:], identd, [], [r_const], "misc")
            dma("sp", maskT[:], maskTd, [], [r_const], "misc")
            dma("sp", rmask[:], rmaskd, [], [r_const], "misc")
            dma("sp", eye4[:], eye4d, [], [r_const], "misc")
            dma("sp", gb[:, 0:2], gbd, [], [r_const], "misc")
            dma("sp", cfg[:], cfgd, [], [r_cfg], "misc")
            dma("sp", brow[:, 0, :], bvod, [], [r_const, r_act], "misc")
            dma("sp", g3bc[:], g3d, [], [r_const], "misc")
            dma("pool", wg[:], wgd.rearrange("(kc p) n -> p kc n", p=128), [], [r_const], "W7", nc_ok=True)
            act_op(ident_b[:], ident_f[:], AF.Copy, [r_const], [r_const])
            dve(lambda e: e.memset(onesm[:], 1.0 / 128.0), [], [r_const])
            dve(lambda e: e.memset(ones4[:], 1.0), [], [r_const])
            dve(lambda e: e.memset(ones2b[:], 1.0), [], [r_const])
            dve(lambda e: e.memset(neghalf[:], -0.5), [], [r_const])
            dve(lambda e: e.tensor_scalar(cvx[:, 0:8], cv[:, BG:BG + 8], 0.5, None, ALU.mult), [r_const], [r_const])
            dve(lambda e: e.tensor_scalar(cvx[:, 8:16], cv[:, HG:HG + 8], 0.5, None, ALU.mult), [r_const], [r_const])
            dve(lambda e: e.tensor_scalar(gb[:, 2:3], gb[:, 1:2], -1.0, None, ALU.mult), [r_const], [r_const])
            dve(lambda e: e.tensor_copy(browb[:, 0, :], brow[:, 0, :]), [r_act], [r_act])
            dve(lambda e: e.tensor_copy(brow[:, 1, :], browb[:, 0, :]), [r_act], [r_act])
            dve(lambda e: e.tensor_tensor(brow[:, 2, :], brow[:, 0, :], brow[:, 1, :], ALU.subtract), [r_act], [r_act])
            dve(lambda e: e.tensor_copy(browb[:, 1, :], brow[:, 2, :]), [r_act], [r_act])
            dma("sp", bias_vo[0:1, :], browb[:, 0, :], [r_act], [r_const], "misc")
            dma("sp", bias_vo[1:2, :], browb[:, 1, :], [r_act], [r_const], "misc")
            dve(lambda e: e.memset(Cst[:], 0.0), [], r_C)
            dve(lambda e: e.memset(mcur[:], NEG), [], [r_mcur])
            dve(lambda e: e.memset(bsum[:], 0.0), [], [r_bsum])
            dve(lambda e: e.memset(qkhalo[:], 0.0), [], [r_qkh])
            dve(lambda e: e.memset(uhalo[:], 0.0), [], [r_uh])
            dve(lambda e: e.memset(ghalo[:], 0.0), [], [r_gh])

            for g in range(7):
                dma("pool", cgin[g][:, :], wsh[g], [], [r_cgin[g]], f"W{g}")

                def fg(e, g=g):
                    return [e.collective_compute("AllGather", ALU.bypass, replica_groups=[list(range(8))],
                                                 ins=[cgin[g][:, :]], outs=[cgout[g][:, :]])]
                S.op("pool", fg, [r_cgin[g]], [r_gath[g]], dma=f"G{g}", unit=1)

            def rs(c0):
                return r_statg[c0 // 8]

            def rstd_from_ss(ss_ap, n, out_ap, r_stat):
                def f1(e):
                    return e.tensor_scalar(out_ap, ss_ap, 1.0 / n, EPS, ALU.mult, ALU.add)
                dve(f1, [r_stat], [r_stat])
                S.op("pool", lambda e: e.tensor_tensor(out_ap, out_ap, neghalf[:, 0:1], ALU.pow),
                     [r_stat, r_const], [r_stat])

            def norm_T(src_buf, j, gcol):
                c0 = stat_col(2)
                r_stat = rs(c0)
                act_op(hn_tm[:], src_buf.t[:], AF.Square, [src_buf.res], [r_hn, r_stat], accum=stat[:, c0:c0 + 1])
                rstd_from_ss(stat[:, c0:c0 + 1], float(D), stat[:, c0 + 1:c0 + 2], r_stat)
                dve(lambda e: e.tensor_scalar(hn_tm[:], src_buf.t[:], stat[:, c0 + 1:c0 + 2], None, ALU.mult),
                    [src_buf.res, r_stat], [r_hn])
                for half in range(2):
                    ps = PS.get()
                    pb = ps.t[:].bitcast(BF16)

                    def ftr(e, pb=pb, half=half):
                        ins = None
                        for i in range(8):
                            kc = half * 8 + i
                            ins = e.transpose(pb[:, i * 128:(i + 1) * 128], hn_tm[:, kc * 128:(kc + 1) * 128], ident_b[:])
                        return ins
                    S.op("pe", ftr, [r_hn, r_const], [ps.res])
                    eng = "act" if half == 0 else "dve"

                    def fev(e, pb=pb, half=half, eng=eng):
                        ins = None
                        for i in range(8):
                            kc = half * 8 + i
                            o = hT[:, kc, j * 128:(j + 1) * 128]
                            if eng == "act":
                                ins = e.activation(out=o, in_=pb[:, i * 128:(i + 1) * 128], func=AF.Copy,
                                                   scale=cv[:, gcol + kc:gcol + kc + 1])
                            else:
                                ins = e.tensor_scalar(o, pb[:, i * 128:(i + 1) * 128], cv[:, gcol + kc:gcol + kc + 1],
                                                      None, ALU.mult)
                        return ins
                    S.op(eng, fev, [ps.res, r_const], [r_hT[j]])

            def load_x(j, sub):
                dma("sp", xm[j].t[:], xh[sub * 128:(sub + 1) * 128, :], [], [xm[j].res], f"x{j}")

            def fm_group(ps, wv, ci, ns):
                TT = 128 * ns
                mm(ps.t[:, 0:TT], [(wv[:, kc, ci * 128:(ci + 1) * 128], hT[:, kc, 0:TT]) for kc in range(KC)],
                   [r_hT[j] for j in range(ns)], [ps.res])

            def gates(ns, negcols):
                TT = 128 * ns
                ps_i, ps_f = PS.get(), PS.get()
                rh = [r_hT[j] for j in range(ns)]
                mm(ps_i.t[0:4, 0:TT], [(wg[:, kc, 0:4], hT[:, kc, 0:TT]) for kc in range(KC)], rh + [r_const], [ps_i.res])
                mm(ps_f.t[0:4, 0:TT], [(wg[:, kc, 4:8], hT[:, kc, 0:TT]) for kc in range(KC)], rh + [r_const], [ps_f.res])
                t_li, t_lf, t_bn, t_u, t_th = TMP.get(), TMP.get(), TMP.get(), TMP.get(), TMP.get()
                li, lf, bn, uu, th = (t.t[0:4, 0:TT] for t in (t_li, t_lf, t_bn, t_u, t_th))
                act_op(li, ps_i.t[0:4, 0:TT], AF.Identity, [ps_i.res, r_const], [t_li.res], bias=gb[:, 0:1])
                if negcols is not None:
                    t_ng = TMP.get()
                    dma("sp", t_ng.t[0:4, 0:TT], negmd[:, negcols:negcols + TT], [], [t_ng.res], "misc")
                    dve(lambda e: e.tensor_tensor(li, li, t_ng.t[0:4, 0:TT], ALU.add), [t_li.res, t_ng.res], [t_li.res])
                act_op(lf, ps_f.t[0:4, 0:TT], AF.Exp, [ps_f.res, r_const], [t_lf.res], bias=gb[:, 2:3], scale=-1.0)
                act_op(lf, lf, AF.Ln, [t_lf.res], [t_lf.res], bias=1.0)
                dve(lambda e: e.tensor_tensor_scan(bn, rmask[:, 0:TT], lf, 0.0, ALU.mult, ALU.add),
                    [t_lf.res, r_const], [t_bn.res])
                dve(lambda e: e.tensor_tensor(li, li, bn, ALU.add), [t_li.res, t_bn.res], [t_li.res])
                dve(lambda e: e.tensor_reduce(gsm[:, 0:ns], li.rearrange("p (c l) -> p c l", l=128), AX.X, ALU.max),
                    [t_li.res], [r_gsm])
                dve(lambda e: e.tensor_copy(gsm[:, 4:4 + ns], bn.rearrange("p (c l) -> p c l", l=128)[:, :, 127]),
                    [t_bn.res], [r_gsm])
                dve(lambda e: e.tensor_tensor_scan(gsm[:, 8:8 + ns], gsm[:, 0:ns], gsm[:, 4:4 + ns], mcur[:, 0:1],
                                                   ALU.max, ALU.subtract), [r_gsm, r_mcur], [r_gsm])
                dve(lambda e: e.tensor_tensor(gsm[:, 12:12 + ns], gsm[:, 8:8 + ns], gsm[:, 4:4 + ns], ALU.add),
                    [r_gsm], [r_gsm])
                dve(lambda e: e.tensor_copy(gsm[:, 16:17], mcur[:, 0:1]), [r_mcur, r_gsm], [r_gsm])
                if ns > 1:
                    dve(lambda e: e.tensor_copy(gsm[:, 17:16 + ns], gsm[:, 8:8 + ns - 1]), [r_gsm], [r_gsm])
                dve(lambda e: e.tensor_tensor(gsm[:, 20:20 + ns], gsm[:, 16:16 + ns], gsm[:, 12:12 + ns], ALU.subtract),
                    [r_gsm], [r_gsm])
                dve(lambda e: e.tensor_scalar(gsm[:, 20:20 + ns], gsm[:, 20:20 + ns], -100.0, None, ALU.max), [r_gsm], [r_gsm])
                act_op(gsm[:, 20:20 + ns], gsm[:, 20:20 + ns], AF.Exp, [r_gsm], [r_gsm])
                dve(lambda e: e.tensor_copy(mcur[:, 0:1], gsm[:, 8 + ns - 1:8 + ns]), [r_gsm], [r_mcur])
                dve(lambda e: e.tensor_reduce(gsm[:, 24:25], gsm[:, 4:4 + ns], AX.X, ALU.add), [r_gsm], [r_gsm])
                dve(lambda e: e.tensor_tensor(bsum[:, 0:1], bsum[:, 0:1], gsm[:, 24:25], ALU.add), [r_gsm, r_bsum], [r_bsum])
                mcb = gsm[:, 12:12 + ns].unsqueeze(2).to_broadcast([4, ns, 128])
                dve(lambda e: e.tensor_tensor(uu.rearrange("p (c l) -> p c l", l=128),
                                              li.rearrange("p (c l) -> p c l", l=128), mcb, ALU.subtract),
                    [t_li.res, r_gsm], [t_u.res])
                dve(lambda e: e.tensor_scalar(uu, uu, -100.0, -LN16, ALU.max, ALU.add), [t_u.res], [t_u.res])
                act_op(uu, uu, AF.Exp, [t_u.res], [t_u.res])
                dve(lambda e: e.tensor_tensor(th.rearrange("p (c l) -> p c l", l=128),
                                              bn.rearrange("p (c l) -> p c l", l=128), mcb, ALU.subtract),
                    [t_bn.res, r_gsm], [t_th.res])
                dve(lambda e: e.tensor_scalar(th, th, 80.0, None, ALU.min), [t_th.res], [t_th.res])
                act_op(th, th, AF.Exp, [t_th.res], [t_th.res])
                ps = PS.get()

                def ftr(e):
                    ins = None
                    for c in range(ns):
                        e.transpose(ps.t[:, c * 8:c * 8 + 4], uu[:, c * 128:(c + 1) * 128], ident_f[0:4, 0:4])
                        ins = e.transpose(ps.t[:, c * 8 + 4:c * 8 + 8], th[:, c * 128:(c + 1) * 128], ident_f[0:4, 0:4])
                    return ins
                S.op("pe", ftr, [t_u.res, t_th.res, r_const], [ps.res])
                dve(lambda e: e.tensor_copy(ut[:, 0:ns, :], ps.t[:, 0:8 * ns].rearrange("p (c k) -> p c k", k=8)),
                    [ps.res], [r_ut])
                t_dd = TMP.get()
                dd = t_dd.t[0:4, 0:4 * ns].rearrange("p (h c) -> p h c", h=4)
                dve(lambda e: e.tensor_tensor(dd, gsm[:, 20:20 + ns].unsqueeze(1).to_broadcast([4, 4, ns]),
                                              eye4[:].unsqueeze(2).to_broadcast([4, 4, ns]), ALU.mult),
                    [r_gsm, r_const], [t_dd.res])
                ps2 = PS.get()
                mm(ps2.t[:, 0:4 * ns], [(ones4[:], t_dd.t[0:4, 0:4 * ns])], [t_dd.res, r_const], [ps2.res])
                dve(lambda e: e.tensor_copy(dmb[:, :, 0:ns], ps2.t[:, 0:4 * ns].rearrange("p (h c) -> p h c", h=4)),
                    [ps2.res], [r_dmb])

            def qk_chunks(ns, slot, blk_first_chunk, maskcols, halo_only=False):
                TT = 128 * ns
                wv = wview(slot)
                for ci in range(4):
                    qc = blk_first_chunk + ci
                    ps = PS.get()
                    mm(ps.t[:, 0:TT], [(wv[:, kc, ci * 128:(ci + 1) * 128], hT[:, kc, 0:TT]) for kc in range(KC)],
                       [r_hT[j] for j in range(ns)] + [slot.res], [ps.res])
                    raw = QKR.get()
                    act_op(raw.t[:, 3:3 + TT], ps.t[:, 0:TT], AF.Identity, [ps.res, r_const], [raw.res],
                           bias=cv[:, BQK + qc:BQK + qc + 1])
                    if maskcols is not None:
                        tm = TMP.get()
                        dma("sp", tm.t[:, 0:TT], tokmaskd[:, maskcols:maskcols + TT], [], [tm.res], "misc")
                        dve(lambda e, raw=raw, tm=tm: e.tensor_tensor(raw.t[:, 3:3 + TT], raw.t[:, 3:3 + TT],
                                                                      tm.t[:, 0:TT], ALU.mult),
                            [raw.res, tm.res], [raw.res])
                    dve(lambda e, raw=raw, qc=qc: e.tensor_copy(raw.t[:, 0:3], qkhalo[:, qc, :]), [r_qkh], [raw.res])
                    dve(lambda e, raw=raw, qc=qc: e.tensor_copy(qkhalo[:, qc, :], raw.t[:, TT:TT + 3]), [raw.res], [r_qkh])
                    if halo_only:
                        continue
                    acc = TMP.get()
                    wc = QKW + qc * 4

                    dve(lambda e, raw=raw, acc=acc, wc=wc, qc=qc: e.tensor_scalar(
                        acc.t[:, 0:TT], raw.t[:, 0:TT], cv[:, wc:wc + 1], cv[:, QKB + qc:QKB + qc + 1], ALU.mult, ALU.add),
                        [raw.res, r_const], [acc.res])
                    for k in range(1, 4):
                        dve(lambda e, raw=raw, acc=acc, wc=wc, k=k: e.scalar_tensor_tensor(
                            acc.t[:, 0:TT], raw.t[:, k:k + TT], cv[:, wc + k:wc + k + 1], acc.t[:, 0:TT], ALU.mult, ALU.add),
                            [raw.res, acc.res, r_const], [acc.res])
                    if qc < 8:
                        act_op(qT_v[:, qc, 0:TT], acc.t[:, 0:TT], AF.Silu, [acc.res], [r_qT])
                    else:
                        act_op(kT_v[:, qc - 8, 0:TT], acc.t[:, 0:TT], AF.Silu, [acc.res], [r_kT])

            def tm_block(ns, slot, bcol, kind):
                wv = wview(slot)
                for j in range(ns):
                    ps = PS.get()
                    pairs = [(hT[:, kc, j * 128:(j + 1) * 128], wv[:, kc, :]) for kc in range(KC)]
                    pairs.append((ones2b[:], bias_vo[:, bcol:bcol + 512]))
                    mm(ps.t[:], pairs, [r_hT[j], slot.res, r_const], [ps.res])
                    if kind < 2:
                        o = vext_v[:, j, 2 * kind:2 * kind + 2, 0:256]
                        act_op(o, ps.t[:].rearrange("p (h e) -> p h e", h=2), AF.Copy, [ps.res], [r_vext])
                    else:
                        c0 = (kind - 2) * 512
                        act_op(oth_v[:, j, c0:c0 + 512], ps.t[:], AF.Tanh, [ps.res], [r_oth], scale=0.5)

            def kv_update(ns, j, h, kt, with_out, qcols):
                tok = slice(j * 128, (j + 1) * 128)
                vh = vext_v[:, j, h, :]
                if with_out:
                    ps_s = PS.get()
                    mm(ps_s.t[:, 0:128], [(kT_v[:, 2 * h + dc, tok], qT_v[:, 2 * h + dc, tok]) for dc in range(2)],
                       [r_kT, r_qT], [ps_s.res])
                    st = STL.get()
                    dve(lambda e: e.scalar_tensor_tensor(st.t[:], ps_s.t[:, 0:128], ut[:, j, h:h + 1], maskT[:],
                                                         ALU.mult, ALU.mult), [ps_s.res, r_ut, r_const], [st.res])
                    cb = CBF.get()
                    act_op(cb.t[:], Cst[:, h, :, :], AF.Copy, [r_C[h], r_dmb], [cb.res], scale=dmb[:, h, j:j + 1])
                    ps_p = PS.get()
                    mm(ps_p.t[:, 0:257], [(qT_v[:, 2 * h, tok], cb.t[:, 0, :]), (qT_v[:, 2 * h + 1, tok], cb.t[:, 1, :]),
                                          (st.t[:], vh)], [r_qT, cb.res, st.res, r_vext], [ps_p.res])
                for dc in range(2):
                    ps_k = PS.get()
                    mm(ps_k.t[:, 0:257], [(kt.t[:, h * 256 + dc * 128:h * 256 + (dc + 1) * 128], vh)],
                       [kt.res, r_vext], [ps_k.res])
                    dve(lambda e, ps_k=ps_k, dc=dc: e.scalar_tensor_tensor(
                        Cst[:, h, dc, :], Cst[:, h, dc, :], dmb[:, h, j:j + 1], ps_k.t[:, 0:257], ALU.mult, ALU.add),
                        [r_C[h], r_dmb, ps_k.res], [r_C[h]])
                if not with_out:
                    return None
                c0 = stat_col(6)
                r_stat = rs(c0)
                act_op(stat[:, c0:c0 + 1], ps_p.t[:, 256:257], AF.Abs, [ps_p.res], [r_stat])
                dve(lambda e: e.tensor_tensor(stat[:, c0:c0 + 1], stat[:, c0:c0 + 1], ut[:, j, 4 + h:5 + h], ALU.max),
                    [r_stat, r_ut], [r_stat])
                dve(lambda e: e.reciprocal(stat[:, c0:c0 + 1], stat[:, c0:c0 + 1]), [r_stat], [r_stat])
                hb = HH.get()
                act_op(hb.t[:], ps_p.t[:, 0:256], AF.Copy, [ps_p.res, r_stat], [hb.res], scale=stat[:, c0:c0 + 1])
                return hb, c0

            def head_norm(hb, c0, j, h, hm):
                r_stat = rs(c0)
                t6 = TMP.get()
                dve(lambda e: e.bn_stats(t6.t[:, 0:6], hb.t[:]), [hb.res], [t6.res])
                dve(lambda e: e.bn_aggr(stat[:, c0 + 1:c0 + 3], t6.t[:, 0:6]), [t6.res], [r_stat])
                dve(lambda e: e.tensor_scalar(stat[:, c0 + 3:c0 + 4], stat[:, c0 + 2:c0 + 3], EPS, None, ALU.add),
                    [r_stat], [r_stat])
                S.op("pool", lambda e: e.tensor_tensor(stat[:, c0 + 3:c0 + 4], stat[:, c0 + 3:c0 + 4], neghalf[:, 0:1],
                                                       ALU.pow), [r_stat, r_const], [r_stat])
                dve(lambda e: e.tensor_scalar(hb.t[:], hb.t[:], stat[:, c0 + 1:c0 + 2], stat[:, c0 + 3:c0 + 4],
                                              ALU.subtract, ALU.mult), [hb.res, r_stat], [hb.res])
                dve(lambda e: e.scalar_tensor_tensor(hm.t[:, h * 256:(h + 1) * 256], oth_v[:, j, h * 256:(h + 1) * 256],
                                                     1.0, hb.t[:], ALU.add, ALU.mult), [r_oth, hb.res], [hm.res])

            def k_token_major(j, with_halfscale=True):
                ps = PS.get()
                pb = ps.t[:].bitcast(BF16)

                def ftr(e):
                    ins = None
                    for c in range(8):
                        ins = e.transpose(pb[:, c * 128:(c + 1) * 128], kT_v[:, c, j * 128:(j + 1) * 128], ident_b[:])
                    return ins
                S.op("pe", ftr, [r_kT, r_const], [ps.res])
                kt = KTL.get()

                def fev(e):
                    ins = None
                    for h in range(4):
                        ins = e.tensor_scalar(kt.t[:, h * 256:(h + 1) * 256], pb[:, h * 256:(h + 1) * 256],
                                              ut[:, j, h:h + 1], None, ALU.mult)
                    return ins
                dve(fev, [ps.res, r_ut], [kt.res])
                return kt

            def set_vext_ones():
                dve(lambda e: e.memset(vext_v[:, :, :, 256:257], 1.0), [], [r_vext])

            load_x(0, 0)
            norm_T(xm[0], 0, G1)
            for kb in range(2):
                slot = wget(6 + kb)
                qk_chunks(1, slot, 8 + 4 * kb, 0, halo_only=True)
            dve(lambda e: e.tensor_copy(khalo0[:], qkhalo[:, 8:16, :]), [r_qkh], [r_kh0])
            for wi in range(NOWN):
                sub0 = 1 + 4 * wi
                for j in range(4):
                    load_x(j, sub0 + j)
                    norm_T(xm[j], j, G1)
                set_vext_ones()
                gates(4, 128 if wi == 0 else None)
                for kb in range(2):
                    slot = wget(6 + kb)
                    qk_chunks(4, slot, 8 + 4 * kb, 128 if wi == 0 else None)
                for vb in range(2):
                    slot = wget(8 + vb)
                    tm_block(4, slot, vb * 512, vb)
                if wi == 1:
                    debug_dump("z_hT", hT[:].rearrange("p k t -> p (k t)"), r_hT)
                    debug_dump("z_gsm", gsm[:, :], r_gsm)
                    debug_dump("z_ut", ut[:].rearrange("p c k -> p (c k)"), r_ut)
                    debug_dump("z_dmb", dmb[:].rearrange("p c k -> p (c k)"), r_dmb)
                    debug_dump("z_kT", kT_v.rearrange("p k t -> p (k t)"), r_kT)
                    debug_dump("z_vext", vext_v.rearrange("p j h e -> p (j h e)"), r_vext)
                for j in range(4):
                    kt = k_token_major(j)
                    for h in range(4):
                        kv_update(4, j, h, kt, False, None)
                debug_dump(f"z_C{wi}", Cst[:].rearrange("p h d e -> p (h d e)"), r_C)

            debug_dump("p0_C", Cst[:].rearrange("p h d e -> p (h d e)"), r_C)
            debug_dump("p0_m", mcur[:, 0:1], r_mcur)
            debug_dump("p0_b", bsum[:, 0:1], r_bsum)
            dma("pool", ccin[0:128, :], Cst[:].rearrange("p h d e -> p (h d e)"), r_C, [r_cc], "cc")
            dma("pool", ccin[128:129, 0:4].rearrange("o (h x) -> (o h) x", x=1), mcur[:, 0:1], [r_mcur], [r_cc], "cc")
            dma("pool", ccin[128:129, 4:8].rearrange("o (h x) -> (o h) x", x=1), bsum[:, 0:1], [r_bsum], [r_cc], "cc")
            r_ccout = Res("ccout")

            def fcc(e):
                return [e.collective_compute("AllGather", ALU.bypass, replica_groups=[list(range(8))],
                                             ins=[ccin[:, :]], outs=[ccout[:, :]])]
            S.op("pool", fcc, [r_cc], [r_ccout], dma="GX", unit=1)
            sc = ccout.rearrange("(r q) n -> q n r", q=129)
            dma("pool", gsm[:, 32:40], sc[128, 0:4, :], [r_ccout, r_gsm], [r_gsm], "cc", nc_ok=True)
            dma("pool", gsm[:, 40:48], sc[128, 4:8, :], [r_ccout, r_gsm], [r_gsm], "cc", nc_ok=True)
            t_d = TMP.get()
            d3 = t_d.t[0:4, 0:64].rearrange("p (j i) -> p j i", j=8)
            dve(lambda e: e.tensor_tensor(d3, gsm[:, 40:48].unsqueeze(1).to_broadcast([4, 8, 8]),
                                          cfg[:, 16:80].rearrange("p (j i) -> p j i", j=8), ALU.mult),
                [r_gsm, r_cfg], [t_d.res])
            dve(lambda e: e.tensor_reduce(gsm[:, 48:56], d3, AX.X, ALU.add), [t_d.res], [r_gsm])
            dve(lambda e: e.tensor_tensor(gsm[:, 48:56], gsm[:, 32:40], gsm[:, 48:56], ALU.subtract), [r_gsm], [r_gsm])
            dve(lambda e: e.tensor_tensor(gsm[:, 48:56], gsm[:, 48:56], cfg[:, 0:8], ALU.add), [r_gsm, r_cfg], [r_gsm])
            dve(lambda e: e.tensor_reduce(gsm[:, 56:57], gsm[:, 48:56], AX.X, ALU.max), [r_gsm], [r_gsm])
            dve(lambda e: e.tensor_scalar(gsm[:, 48:56], gsm[:, 48:56], gsm[:, 56:57], -100.0, ALU.subtract, ALU.max), [r_gsm], [r_gsm])
            act_op(gsm[:, 48:56], gsm[:, 48:56], AF.Exp, [r_gsm], [r_gsm])
            dve(lambda e: e.tensor_tensor(gsm[:, 48:56], gsm[:, 48:56], cfg[:, 8:16], ALU.mult), [r_gsm, r_cfg], [r_gsm])
            dve(lambda e: e.tensor_copy(mcur[:, 0:1], gsm[:, 56:57]), [r_gsm], [r_mcur])
            t_cd = TMP.get()
            cd = t_cd.t[0:4, 0:32].rearrange("p (h r) -> p h r", h=4)
            dve(lambda e: e.tensor_tensor(cd, gsm[:, 48:56].unsqueeze(1).to_broadcast([4, 4, 8]),
                                          eye4[:].unsqueeze(2).to_broadcast([4, 4, 8]), ALU.mult),
                [r_gsm, r_const], [t_cd.res])
            ps_c = PS.get()
            mm(ps_c.t[:, 0:32], [(ones4[:], t_cd.t[0:4, 0:32])], [t_cd.res, r_const], [ps_c.res])
            t_cb = TMP.get()
            dve(lambda e: e.tensor_copy(t_cb.t[:, 0:32], ps_c.t[:, 0:32]), [ps_c.res], [t_cb.res])
            dve(lambda e: e.memset(Cst[:], 0.0), [], r_C)
            for r in range(8):
                stg = cstage_v[r % 2]
                dma("pool", stg, ccout[r * 129:r * 129 + 128, :], [r_ccout], [r_act], f"cs{r % 2}")
                for h in range(4):
                    dve(lambda e, stg=stg, h=h, r=r: e.scalar_tensor_tensor(
                        Cst[:, h, :, :].rearrange("p d e -> p (d e)"), stg[:, h * 514:(h + 1) * 514],
                        t_cb.t[:, h * 8 + r:h * 8 + r + 1], Cst[:, h, :, :].rearrange("p d e -> p (d e)"),
                        ALU.mult, ALU.add), [r_act, t_cb.res, r_C[h]], [r_C[h]])
            debug_dump("ex_C", Cst[:].rearrange("p h d e -> p (h d e)"), r_C)
            debug_dump("ex_m", mcur[:, 0:1], r_mcur)
            debug_dump("ex_g", gsm[:, :], r_gsm)
            dve(lambda e: e.memset(qkhalo[:, 0:8, :], 0.0), [], [r_qkh])
            dve(lambda e: e.tensor_copy(qkhalo[:, 8:16, :], khalo0[:]), [r_kh0], [r_qkh])

            def mixer(ns, sub0, masked):
                TT = 128 * ns
                mcol = sub0 * 128 if masked else None
                dd = (sub0 == 3)
                for j in range(ns):
                    load_x(j, sub0 + j)
                    norm_T(xm[j], j, G1)
                if dd:
                    debug_dump("hT", hT[:].rearrange("p k t -> p (k t)"), r_hT)
                set_vext_ones()
                gates(ns, mcol)
                if dd:
                    debug_dump("gsm", gsm[:, :], r_gsm)
                    debug_dump("ut", ut[:].rearrange("p c k -> p (c k)"), r_ut)
                    debug_dump("dmb", dmb[:].rearrange("p c k -> p (c k)"), r_dmb)
                for b in range(4):
                    slot = wget(4 + b)
                    qk_chunks(ns, slot, 4 * b, mcol)
                for vb in range(4):
                    slot = wget(8 + vb)
                    tm_block(ns, slot, vb * 512, vb)
                if dd:
                    debug_dump("qT", qT_v.rearrange("p k t -> p (k t)"), r_qT)
                    debug_dump("kT", kT_v.rearrange("p k t -> p (k t)"), r_kT)
                    debug_dump("vext", vext_v.rearrange("p j h e -> p (j h e)"), r_vext)
                    debug_dump("oth", oth_v.rearrange("p j n -> p (j n)"), r_oth)
                for j in range(ns):
                    kt = k_token_major(j)
                    hm = HMT.get()
                    for h in range(4):
                        hb, c0 = kv_update(ns, j, h, kt, True, None)
                        head_norm(hb, c0, j, h, hm)
                    ps = PS.get()
                    pb = ps.t[:].bitcast(BF16)

                    def ftr(e, pb=pb, hm=hm):
                        ins = None
                        for c in range(8):
                            ins = e.transpose(pb[:, c * 128:(c + 1) * 128], hm.t[:, c * 128:(c + 1) * 128], ident_b[:])
                        return ins
                    S.op("pe", ftr, [hm.res, r_const], [ps.res])

                    def fev(e, pb=pb, j=j):
                        ins = None
                        for c in range(8):
                            ins = e.activation(out=mixT_v[:, 8 + c, j * 128:(j + 1) * 128], in_=pb[:, c * 128:(c + 1) * 128],
                                               func=AF.Copy, scale=cvx[:, 8 + c:9 + c])
                        return ins
                    S.op("act", fev, [ps.res, r_const], [r_mix])
                for half in range(2):
                    rh = [r_hT[j] for j in range(ns)]
                    sa = wget(0 + half)
                    wa = wview(sa)
                    ps_as = []
                    for ci in range(4):
                        ps_a = PS.get(pin=True)
                        mm(ps_a.t[:, 0:TT], [(wa[:, kc, ci * 128:(ci + 1) * 128], hT[:, kc, 0:TT]) for kc in range(KC)],
                           rh + [sa.res], [ps_a.res])
                        ps_as.append(ps_a)
                    sg = wget(2 + half)
                    wgv = wview(sg)
                    for ci in range(4):
                        c = half * 4 + ci
                        ps_a = ps_as[ci]
                        ps_g = PS.get()
                        mm(ps_g.t[:, 0:TT], [(wgv[:, kc, ci * 128:(ci + 1) * 128], hT[:, kc, 0:TT]) for kc in range(KC)],
                           rh + [sg.res], [ps_g.res])
                        tg = TMP.get()
                        act_op(tg.t[:, 0:TT], ps_g.t[:, 0:TT], AF.Tanh, [ps_g.res, r_const], [tg.res],
                               bias=cvx[:, c:c + 1], scale=0.5)
                        dve(lambda e, tg=tg: e.tensor_scalar(tg.t[:, 0:TT], tg.t[:, 0:TT], 0.5, 0.5, ALU.mult, ALU.add),
                            [tg.res], [tg.res])
                        up = UPR.get()
                        dve(lambda e, up=up, tg=tg, ps_a=ps_a, c=c: e.scalar_tensor_tensor(
                            up.t[:, 30:30 + TT], ps_a.t[:, 0:TT], cv[:, BA + c:BA + c + 1], tg.t[:, 0:TT], ALU.add, ALU.mult),
                            [ps_a.res, tg.res, r_const], [up.res])
                        PS.unpin(ps_a)
                        if mcol is not None:
                            tm = TMP.get()
                            dma("sp", tm.t[:, 0:TT], tokmaskd[:, mcol:mcol + TT], [], [tm.res], "misc")
                            dve(lambda e, up=up, tm=tm: e.tensor_tensor(up.t[:, 30:30 + TT], up.t[:, 30:30 + TT],
                                                                        tm.t[:, 0:TT], ALU.mult), [up.res, tm.res], [up.res])
                        dve(lambda e, up=up, c=c: e.tensor_copy(up.t[:, 0:30], uhalo[:, c, :]), [r_uh], [up.res])
                        dve(lambda e, up=up, c=c: e.tensor_copy(uhalo[:, c, :], up.t[:, TT:TT + 30]), [up.res], [r_uh])
                        xc = TMP.get()
                        wc = CW + c * 31
                        dve(lambda e, up=up, xc=xc, wc=wc, c=c: e.tensor_scalar(
                            xc.t[:, 0:TT], up.t[:, 0:TT], cv[:, wc:wc + 1], cv[:, CB + c:CB + c + 1], ALU.mult, ALU.add),
                            [up.res, r_const], [xc.res])
                        for k in range(1, 31):
                            dve(lambda e, up=up, xc=xc, wc=wc, k=k: e.scalar_tensor_tensor(
                                xc.t[:, 0:TT], up.t[:, k:k + TT], cv[:, wc + k:wc + k + 1], xc.t[:, 0:TT], ALU.mult, ALU.add),
                                [up.res, xc.res, r_const], [xc.res])
                        sq = TMP.get()
                        act_op(sq.t[:, 0:TT], xc.t[:, 0:TT], AF.Square, [xc.res], [sq.res])
                        ps_m, ps_e = PS.get(), PS.get()
                        mm(ps_m.t[:, 0:TT], [(onesm[:], xc.t[:, 0:TT])], [xc.res, r_const], [ps_m.res])
                        mm(ps_e.t[:, 0:TT], [(onesm[:], sq.t[:, 0:TT])], [sq.res, r_const], [ps_e.res])
                        act_op(sq.t[:, 0:TT], ps_m.t[:, 0:TT], AF.Square, [ps_m.res], [sq.res])
                        dve(lambda e, sq=sq, ps_e=ps_e: e.scalar_tensor_tensor(
                            sq.t[:, 0:TT], ps_e.t[:, 0:TT], EPS, sq.t[:, 0:TT], ALU.add, ALU.subtract),
                            [ps_e.res, sq.res], [sq.res])
                        S.op("pool", lambda e, sq=sq: e.tensor_tensor(sq.t[:, 0:TT], sq.t[:, 0:TT], neghalf[:, 0:TT], ALU.pow),
                             [sq.res, r_const], [sq.res])
                        dve(lambda e, xc=xc, ps_m=ps_m: e.tensor_tensor(xc.t[:, 0:TT], xc.t[:, 0:TT], ps_m.t[:, 0:TT],
                                                                        ALU.subtract), [xc.res, ps_m.res], [xc.res])
                        dve(lambda e, xc=xc, sq=sq: e.tensor_tensor(xc.t[:, 0:TT], xc.t[:, 0:TT], sq.t[:, 0:TT], ALU.mult),
                            [xc.res, sq.res], [xc.res])
                        act_op(mixT_v[:, c, 0:TT], xc.t[:, 0:TT], AF.Silu, [xc.res, r_const], [r_mix],
                               bias=cv[:, GNB + c:GNB + c + 1], scale=cv[:, GNG + c:GNG + c + 1])
                if dd:
                    debug_dump("mixT", mixT_v.rearrange("p k t -> p (k t)"), r_mix)
                    debug_dump("C1", Cst[:].rearrange("p h d e -> p (h d e)"), r_C)
                for cb in range(4):
                    slot = wget(12 + cb)
                    wv = wview(slot)
                    for j in range(ns):
                        ps = PS.get()
                        mm(ps.t[:], [(mixT_v[:, kc, j * 128:(j + 1) * 128], wv[:, kc, :]) for kc in range(KC)],
                           [r_mix, slot.res], [ps.res])
                        dve(lambda e, ps=ps, j=j, cb=cb: e.tensor_tensor(
                            xm[j].t[:, cb * 512:(cb + 1) * 512], xm[j].t[:, cb * 512:(cb + 1) * 512], ps.t[:], ALU.add),
                            [ps.res, xm[j].res], [xm[j].res])

            def ffn(ns, halo_only, out_row0, gmaskcols):
                TT = 128 * ns
                if not halo_only and out_row0 == 0:
                    debug_dump("xmid", xm[0].t[:], xm[0].res)
                    debug_dump("ghalo", ghalo[:].rearrange("p c k -> p (c k)"), r_gh)
                for j in range(ns):
                    norm_T(xm[j], j, G2)
                rh = [r_hT[j] for j in range(ns)]
                for blk in range(11):
                    sg = wget(16 + blk)
                    wgt_v = wview(sg)
                    accl = []
                    for ci in range(4):
                        jj = blk * 4 + ci
                        ps_g = PS.get()
                        mm(ps_g.t[:, 0:TT], [(wgt_v[:, kc, ci * 128:(ci + 1) * 128], hT[:, kc, 0:TT]) for kc in range(KC)],
                           rh + [sg.res], [ps_g.res])
                        raw = QKR.get()
                        act_op(raw.t[:, 2:2 + TT], ps_g.t[:, 0:TT], AF.Copy, [ps_g.res], [raw.res])
                        dve(lambda e, raw=raw, jj=jj: e.tensor_copy(raw.t[:, 0:2], ghalo[:, jj, :]), [r_gh], [raw.res])
                        dve(lambda e, raw=raw, jj=jj: e.tensor_copy(ghalo[:, jj, :], raw.t[:, TT:TT + 2]), [raw.res], [r_gh])
                        if halo_only:
                            continue
                        acc = TMP.get()
                        accl.append(acc)
                        wc = FW + jj * 3
                        dve(lambda e, raw=raw, acc=acc, wc=wc, jj=jj: e.tensor_scalar(
                            acc.t[:, 0:TT], raw.t[:, 0:TT], cv[:, wc:wc + 1], cv[:, FB + jj:FB + jj + 1], ALU.mult, ALU.add),
                            [raw.res, r_const], [acc.res])
                        for k in range(1, 3):
                            dve(lambda e, raw=raw, acc=acc, wc=wc, k=k: e.scalar_tensor_tensor(
                                acc.t[:, 0:TT], raw.t[:, k:k + TT], cv[:, wc + k:wc + k + 1], acc.t[:, 0:TT], ALU.mult, ALU.add),
                                [raw.res, acc.res, r_const], [acc.res])
                        act_op(acc.t[:, 0:TT], acc.t[:, 0:TT], AF.Silu, [acc.res], [acc.res])
                    if halo_only:
                        continue
                    sv = wget(27 + blk)
                    wvl_v = wview(sv)
                    for ci in range(4):
                        jj = blk * 4 + ci
                        acc = accl[ci]
                        ps_v = PS.get()
                        mm(ps_v.t[:, 0:TT], [(wvl_v[:, kc, ci * 128:(ci + 1) * 128], hT[:, kc, 0:TT]) for kc in range(KC)],
                           rh + [sv.res], [ps_v.res])
                        dve(lambda e, acc=acc, ps_v=ps_v, jj=jj: e.tensor_tensor(act_v[:, jj, 0:TT], acc.t[:, 0:TT],
                                                                                 ps_v.t[:, 0:TT], ALU.mult),
                            [acc.res, ps_v.res], [r_act])
                if halo_only:
                    if gmaskcols is not None:
                        tm = TMP.get()
                        dma("sp", tm.t[:, 0:2], tokmaskd[:, gmaskcols:gmaskcols + 2], [], [tm.res], "misc")
                        dve(lambda e: e.tensor_tensor(ghalo[:], ghalo[:], tm.t[:, 0:2].unsqueeze(1).to_broadcast([128, NFC, 2]),
                                                      ALU.mult), [tm.res, r_gh], [r_gh])
                    return
                if out_row0 == 0:
                    debug_dump("act", act_v.rearrange("p k t -> p (k t)"), r_act)
                for cb in range(4):
                    accs = [PS.get() for _ in range(ns)]
                    for q in range(4):
                        slot = wget(38 + cb * 4 + q)
                        wv = wview(slot, 11)
                        for j in range(ns):
                            mm(accs[j].t[:], [(act_v[:, q * 11 + i, j * 128:(j + 1) * 128], wv[:, i, :]) for i in range(11)],
                               [r_act, slot.res], [accs[j].res], first=(q == 0), last=(q == 3))
                    for j in range(ns):
                        dve(lambda e, j=j, cb=cb, a=accs[j]: e.tensor_tensor(
                            xm[j].t[:, cb * 512:(cb + 1) * 512], xm[j].t[:, cb * 512:(cb + 1) * 512], a.t[:], ALU.add),
                            [accs[j].res, xm[j].res], [xm[j].res])
                for j in range(ns):
                    c0 = stat_col(2)
                    r_stat = rs(c0)
                    act_op(hn_tm[:], xm[j].t[:], AF.Square, [xm[j].res], [r_hn, r_stat], accum=stat[:, c0:c0 + 1])
                    rstd_from_ss(stat[:, c0:c0 + 1], float(D), stat[:, c0 + 1:c0 + 2], r_stat)
                    dve(lambda e, j=j, c0=c0: e.scalar_tensor_tensor(xm[j].t[:], xm[j].t[:], stat[:, c0 + 1:c0 + 2], g3bc[:],
                                                                     ALU.mult, ALU.mult), [xm[j].res, r_stat, r_const], [xm[j].res])
                    ro = Res("out")
                    r_out.append(ro)
                    dma("sp", outd[out_row0 + j * 128:out_row0 + (j + 1) * 128, :], xm[j].t[:], [xm[j].res], [ro], f"x{j}")

            mixer(2, 1, True)
            ffn(2, True, None, 382)
            for st in range(NOWN):
                mixer(4, 3 + 4 * st, False)
                ffn(4, False, st * 512, None)
            S.op("sp", lambda e: None, r_out, [])

        S0 = Sched()
        S0.dry = True
        emit(S0)
        WS.pos = 0
        S1 = Sched()
        emit(S1)

        with nc.Block() as block:
            @block.tensor
            def _(e):
                S1.replay("pe", e, sems)

            @block.scalar
            def _(e):
                S1.replay("act", e, sems)

            @block.vector
            def _(e):
                S1.replay("dve", e, sems)

            @block.gpsimd
            def _(e):
                S1.replay("pool", e, sems)

            @block.sync
            def _(e):
                S1.replay("sp", e, sems)
    return nc


def _colvec(a, nchunk):
    return np.ascontiguousarray(np.asarray(a, np.float32).reshape(nchunk, 128).T)


def _prep(inputs, NOWN):
    f = lambda k: np.asarray(inputs[k], np.float32)
    x = f("x")
    TOK = 512 * NOWN
    b_in = f("b_in")
    cv = np.zeros((128, NCV), np.float32)
    cv[:, G1:G1 + 16] = _colvec(f("norm_mix_g"), 16)
    cv[:, G2:G2 + 16] = _colvec(f("norm_ffn_g"), 16)
    cv[:, BA:BA + 8] = _colvec(b_in[0:1024], 8)
    cv[:, BG:BG + 8] = _colvec(b_in[1024:2048], 8)
    cv[:, BQK:BQK + 16] = _colvec(b_in[2048:4096], 16)
    cv[:, CW:CW + 248] = f("conv_dw_w").reshape(31, 8, 128).transpose(2, 1, 0).reshape(128, 248)
    cv[:, CB:CB + 8] = _colvec(f("conv_dw_b"), 8)
    cv[:, GNG:GNG + 8] = _colvec(f("conv_gn_g"), 8)
    cv[:, GNB:GNB + 8] = _colvec(f("conv_gn_b"), 8)
    cv[:, QKW:QKW + 64] = f("qk_conv_w").reshape(4, 16, 128).transpose(2, 1, 0).reshape(128, 64)
    cv[:, QKB:QKB + 16] = _colvec(f("qk_conv_b"), 16)
    cv[:, FW:FW + 132] = f("ffn_conv_w").reshape(3, NFC, 128).transpose(2, 1, 0).reshape(128, 132)
    cv[:, FB:FB + NFC] = _colvec(f("ffn_conv_b"), NFC)
    cv[:, HG:HG + 8] = _colvec(f("mlstm_hn_g"), 8)
    w_in, w_out, w_up, w_down = f("w_in"), f("w_out"), f("w_up"), f("w_down")

    def blk_host(i):
        if i < 12:
            m, nk = w_in[:, i * 512:(i + 1) * 512], 16
        elif i < 16:
            m, nk = w_out[:, (i - 12) * 512:(i - 11) * 512], 16
        elif i < 38:
            m, nk = w_up[:, (i - 16) * 512:(i - 15) * 512], 16
        else:
            cb, q = divmod(i - 38, 4)
            m, nk = w_down[q * 1408:(q + 1) * 1408, cb * 512:(cb + 1) * 512], 11
        o = np.zeros((128, 8192), np.float32)
        o[:, 0:nk * 512] = m.reshape(nk, 128, 512).transpose(1, 0, 2).reshape(128, nk * 512)
        return o
    common = {
        "cv": cv,
        "g3bc": np.ascontiguousarray(np.broadcast_to(f("norm_final_g")[None, :], (128, D))),
        "bvo": np.ascontiguousarray(b_in[4096:6144][None, :]),
        "gbias": np.ascontiguousarray(np.stack([b_in[6144:6148], b_in[6148:6152]], axis=1)),
        "wgate": np.ascontiguousarray(w_in[:, 6144:6152]),
        "ident": np.eye(128, dtype=np.float32),
        "maskT": np.triu(np.ones((128, 128), np.float32)),
        "rmask": np.ascontiguousarray(np.broadcast_to((np.arange(512) % 128 != 0).astype(np.float32)[None, :], (4, 512))),
        "eye4": np.eye(4, dtype=np.float32),
    }
    maps = []
    for c in range(8):
        b, seg = divmod(c, 4)
        t0 = seg * TOK
        lo = t0 - 384
        xh = np.zeros((TOK + 384, D), np.float32)
        s = max(lo, 0)
        xh[s - lo:] = x[b, s:t0 + TOK]
        valid = ((lo + np.arange(640)) >= 0)
        tokmask = np.ascontiguousarray(np.broadcast_to(valid.astype(np.float32)[None, :], (128, 640)))
        negm = np.ascontiguousarray(np.broadcast_to(np.where(valid, 0.0, NEG).astype(np.float32)[None, :], (4, 640)))
        cfg = np.zeros((80,), np.float32)
        for j in range(8):
            ok = (j // 4 == b) and (j % 4 < seg)
            cfg[j] = 0.0 if ok else NEG
            cfg[8 + j] = 1.0 if ok else 0.0
            if ok:
                for i in range(8):
                    if i // 4 == b and j < i < c:
                        cfg[16 + j * 8 + i] = 1.0
        wsh = np.zeros((7, 128, 8192), np.float32)
        for g in range(7):
            if g * 8 + c < len(BLK_ORDER):
                wsh[g] = blk_host(BLK_ORDER[g * 8 + c])
        m = dict(common)
        m.update({"xh": xh, "tokmask": tokmask, "negm": negm, "wsh": wsh,
                  "cfg": np.ascontiguousarray(np.broadcast_to(cfg[None, :], (4, 80)))})
        maps.append(m)
    return maps


_NC_CACHE = {}


def _run(inputs, NOWN, dbg=None):
    maps = _prep(inputs, NOWN)
    key = (NOWN, tuple(n for n, _ in (dbg or [])))
    if key not in _NC_CACHE:
        _NC_CACHE[key] = build(NOWN, dbg)
    nc = _NC_CACHE[key]
    res = run_bass_kernel_spmd(nc, maps, core_ids=list(range(8)))
    TOK = 512 * NOWN
    out = np.zeros((2, 4 * TOK, D), np.float32)
    for c in range(8):
        b, seg = divmod(c, 4)
        out[b, seg * TOK:(seg + 1) * TOK] = res.results[c]["out"]
    if dbg:
        return out, res.results
    return out


def kernel(**inputs):
    return _run(inputs, 8)
```

```python
import numpy as np
from contextlib import ExitStack
import concourse.bass as bass
import concourse.mybir as mybir
from concourse.bass_utils import run_bass_kernel_spmd

F32 = mybir.dt.float32
BF16 = mybir.dt.bfloat16
AF = mybir.ActivationFunctionType
ALU = mybir.AluOpType
AX = mybir.AxisListType

D = 2048
KC = 16
DIN = 6152
DFF = 5632
NFC = 44
NEG = -1.0e4
EPS = 1e-6
LN16 = float(np.log(16.0))

G1, G2, BA, BG, BQK, CW, CB, GNG, GNB, QKW, QKB, FW, FB, HG = 0, 16, 32, 40, 48, 64, 312, 320, 328, 336, 400, 416, 548, 592
NCV = 600
NBLK = 54
BLK_ORDER = [6, 7, 8, 9, 4, 5, 10, 11, 0, 2, 1, 3, 12, 13, 14, 15]
for _b in range(11):
    BLK_ORDER += [16 + _b, 27 + _b]
BLK_ORDER += list(range(38, 54))


class Res:
    __slots__ = ("name", "w", "r", "alias")

    def __init__(self, name):
        self.name = name
        self.w = None
        self.r = {}
        self.alias = ()


class Buf:
    def __init__(self, t, name):
        self.t = t
        self.res = Res(name)


class RR:
    def __init__(self, bufs):
        self.bufs = bufs
        self.i = 0
        self.pinned = set()

    def get(self, pin=False):
        while True:
            b = self.bufs[self.i % len(self.bufs)]
            self.i += 1
            if id(b) not in self.pinned:
                break
        if pin:
            self.pinned.add(id(b))
        return b

    def unpin(self, b):
        self.pinned.discard(id(b))


class Sched:
    ENG = ("pe", "act", "dve", "pool", "sp")

    def __init__(self):
        self.plan = {e: [] for e in self.ENG}
        self.cnt = {}
        self.seen = {e: {} for e in self.ENG}
        self.last_dma = {}
        self.dry = False

    def op(self, eng, fn, reads=(), writes=(), dma=None, ndma=1, unit=16):
        if self.dry:
            return
        deps = {}

        def add(tok, raw):
            if tok is None:
                return
            k, v = tok
            if k == eng and (not raw or eng == "pe"):
                return
            if deps.get(k, 0) < v:
                deps[k] = v

        for r in reads:
            add(r.w, True)
        for w in writes:
            add(w.w, False)
            for k, v in w.r.items():
                add((k, v), False)
            for a in w.alias:
                add(a.w, False)
                for k, v in a.r.items():
                    add((k, v), False)
        if dma is not None:
            add(self.last_dma.get(dma), True)
        waits = []
        for k, v in deps.items():
            if self.seen[eng].get(k, 0) >= v:
                continue
            self.seen[eng][k] = v
            waits.append((k, v))
        if dma is not None:
            k, inc = dma, unit * ndma
        else:
            k, inc = eng, 1
        self.cnt[k] = self.cnt.get(k, 0) + inc
        tok = (k, self.cnt[k])
        if dma is not None:
            self.last_dma[dma] = tok
        self.plan[eng].append([waits, fn, k, unit if dma is not None else 0, tok[1]])
        for r in reads:
            if r.r.get(k, 0) < tok[1]:
                r.r[k] = tok[1]
        for w in writes:
            w.w = tok
            w.r = {}

    def prune(self):
        needed = set()
        for eng in self.ENG:
            for waits, fn, k, unit, v in self.plan[eng]:
                for wk, wv in waits:
                    needed.add((wk, wv))
        newval = {}
        for eng in self.ENG:
            c = 0
            for ent in self.plan[eng]:
                waits, fn, k, unit, v = ent
                if unit:
                    continue
                if (k, v) in needed:
                    c += 1
                    newval[(k, v)] = c
                    ent.append(True)
                else:
                    ent.append(False)
        for eng in self.ENG:
            for ent in self.plan[eng]:
                ent[0] = [(wk, newval.get((wk, wv), wv)) for wk, wv in ent[0]]

    def replay(self, eng, e, sems):
        for ent in self.plan[eng]:
            waits, fn, k, unit = ent[0], ent[1], ent[2], ent[3]
            for wk, wv in waits:
                e.wait_ge(sems[wk], wv)
            r = fn(e)
            if unit:
                for ins in r:
                    ins.then_inc(sems[k], unit)
            elif r is not None and ent[5]:
                r.then_inc(sems[k], 1)


class WStream:
    def __init__(self):
        self.order = []
        self.pos = 0
        self.issued = 0
        self.slots = None
        self.loaded = {}

    def get(self, S, blk, issue):
        if S.dry:
            self.order.append(blk)
            return self.slots[0]
        assert self.order[self.pos] == blk, (self.pos, self.order[self.pos], blk)
        while self.issued < min(len(self.order), self.pos + 2):
            i = self.issued
            slot = self.slots[i % len(self.slots)]
            issue(self.order[i], slot)
            self.loaded[i] = slot
            self.issued += 1
        slot = self.loaded.pop(self.pos)
        self.pos += 1
        return slot


def build(NOWN, dbg=None):
    dbg = dbg or []
    nc = bass.Bass("TRN2", target_bir_lowering=False)
    NSUBT = 3 + 4 * NOWN
    ROWS = 128 * NSUBT

    def din(name, shape, dt=F32):
        return nc.dram_tensor(name, shape, dt, kind="ExternalInput").ap()

    xh = din("xh", [ROWS, D])
    wsh = din("wsh", [7, 128, 8192])
    cvd = din("cv", [128, NCV])
    g3d = din("g3bc", [128, D])
    bvod = din("bvo", [1, 2048])
    gbd = din("gbias", [4, 2])
    wgd = din("wgate", [D, 8])
    identd = din("ident", [128, 128])
    maskTd = din("maskT", [128, 128])
    rmaskd = din("rmask", [4, 512])
    eye4d = din("eye4", [4, 4])
    tokmaskd = din("tokmask", [128, 640])
    negmd = din("negm", [4, 640])
    cfgd = din("cfg", [4, 80])
    outd = nc.dram_tensor("out", [512 * NOWN, D], F32, kind="ExternalOutput").ap()
    dbgd = {}
    for name, shape in dbg:
        dbgd[name] = nc.dram_tensor("dbg_" + name, list(shape), F32, kind="ExternalOutput").ap()
    cgin = [nc.dram_tensor(f"cgin{g}", [128, 8192], BF16) for g in range(7)]
    cgout = [nc.dram_tensor(f"cgout{g}", [8 * 128, 8192], BF16) for g in range(7)]
    ccin = nc.dram_tensor("ccin", [129, 2056], F32)
    ccout = nc.dram_tensor("ccout", [8 * 129, 2056], F32)

    es = ExitStack()
    with es:
        def sb(name, shape, dt=F32):
            return es.enter_context(nc.sbuf_tensor("sb_" + name, shape, dt))

        ws_t = [sb(f"ws{i}", [128, 8192], BF16) for i in range(2)]
        xm_t = [sb(f"xm{i}", [128, D]) for i in range(4)]
        hT = sb("hT", [128, KC, 512], BF16)
        hn_tm = sb("hn_tm", [128, D], BF16)
        Cst = sb("Cst", [128, 4, 2, 257])
        cv = sb("cv", [128, NCV])
        cvx = sb("cvx", [128, 16])
        g3bc = sb("g3bc", [128, D])
        ident_f = sb("ident_f", [128, 128])
        ident_b = sb("ident_b", [128, 128], BF16)
        maskT = sb("maskT", [128, 128])
        onesm = sb("onesm", [128, 128])
        ones4 = sb("ones4", [4, 128])
        ones2b = sb("ones2b", [2, 128], BF16)
        eye4 = sb("eye4", [4, 4])
        rmask = sb("rmask", [4, 512])
        neghalf = sb("neghalf", [128, 512])
        wg = sb("wg", [128, KC, 8], BF16)
        bias_vo = sb("bias_vo", [2, 2048], BF16)
        gb = sb("gb", [4, 4])
        qkhalo = sb("qkhalo", [128, 16, 3])
        khalo0 = sb("khalo0", [128, 8, 3])
        uhalo = sb("uhalo", [128, 8, 30])
        ghalo = sb("ghalo", [128, NFC, 2])
        mcur = sb("mcur", [4, 1])
        bsum = sb("bsum", [4, 1])
        stat = sb("stat", [128, 128])
        gsm = sb("gsm", [4, 64])
        cfg = sb("cfg", [4, 80])
        ovl = sb("ovl", [128, 12352])
        qkraw_t = [sb(f"qkraw{i}", [128, 515]) for i in range(2)]
        upre_t = [sb(f"upre{i}", [128, 542]) for i in range(2)]
        tmp_t = [sb(f"tmp{i}", [128, 512]) for i in range(8)]
        ktil_t = [sb(f"ktil{i}", [128, 1024], BF16) for i in range(2)]
        St_t = [sb(f"St{i}", [128, 128], BF16) for i in range(2)]
        Cbf_t = [sb(f"Cbf{i}", [128, 2, 257], BF16) for i in range(2)]
        hh_t = [sb(f"hh{i}", [128, 256]) for i in range(2)]
        hmt_t = [sb(f"hmt{i}", [128, 1024], BF16) for i in range(2)]
        ut = sb("ut", [128, 4, 8])
        dmb = sb("dmb", [128, 4, 8])
        pst = [es.enter_context(nc.psum_tensor(f"ps{i}", [128, 512], F32)) for i in range(8)]

        ob = ovl[:].bitcast(BF16)
        act_v = ob[:, 0:NFC * 512].rearrange("p (c t) -> p c t", c=NFC)
        mixT_v = ob[:, 0:8192].rearrange("p (c t) -> p c t", c=16)
        qT_v = ob[:, 8192:12288].rearrange("p (c t) -> p c t", c=8)
        kT_v = ob[:, 12288:16384].rearrange("p (c t) -> p c t", c=8)
        vext_v = ob[:, 16384:16384 + 4112].rearrange("p (j h e) -> p j h e", j=4, h=4)
        oth_v = ob[:, 20512:20512 + 4096].rearrange("p (j n) -> p j n", j=4)
        brow = ovl[0:1, 0:8192].rearrange("p (r n) -> p r n", r=4)
        browb = ovl[0:1, 8192:10240].bitcast(BF16).rearrange("p (r n) -> p r n", r=2)
        cstage_v = [ovl[:, 0:2056], ovl[:, 2056:4112]]

        sem_names = list(Sched.ENG) + [f"W{i}" for i in range(8)] + ["ws0", "ws1", "x0", "x1", "x2", "x3",
                                                                      "misc", "cc", "cs0", "cs1", "dbg", "GX"] + [f"G{i}" for i in range(7)]
        sems = {n: es.enter_context(nc.semaphore("s_" + n)) for n in sem_names}

        WS = WStream()

        def emit(S):
            ws = [Buf(ws_t[i], f"ws{i}") for i in range(2)]
            WS.slots = ws
            xm = [Buf(xm_t[i], f"xm{i}") for i in range(4)]
            PS = RR([Buf(pst[i], f"ps{i}") for i in range(8)])
            TMP = RR([Buf(tmp_t[i], f"tmp{i}") for i in range(8)])
            QKR = RR([Buf(qkraw_t[i], f"qkraw{i}") for i in range(2)])
            UPR = RR([Buf(upre_t[i], f"upre{i}") for i in range(2)])
            KTL = RR([Buf(ktil_t[i], f"ktil{i}") for i in range(2)])
            STL = RR([Buf(St_t[i], f"St{i}") for i in range(2)])
            CBF = RR([Buf(Cbf_t[i], f"Cbf{i}") for i in range(2)])
            HH = RR([Buf(hh_t[i], f"hh{i}") for i in range(2)])
            HMT = RR([Buf(hmt_t[i], f"hmt{i}") for i in range(2)])
            r_hT = [Res(f"hT{j}") for j in range(4)]
            r_hn = Res("hn_tm")
            r_C = [Res(f"C{h}") for h in range(4)]
            r_const = Res("const")
            r_qkh, r_kh0, r_uh, r_gh = Res("qkhalo"), Res("khalo0"), Res("uhalo"), Res("ghalo")
            r_mcur, r_bsum, r_gsm, r_cfg = Res("mcur"), Res("bsum"), Res("gsm"), Res("cfg")
            r_ut, r_dmb = Res("ut"), Res("dmb")
            r_act, r_mix, r_qT, r_kT, r_vext, r_oth = (Res("act"), Res("mixT"), Res("qT"), Res("kT"),
                                                       Res("vext"), Res("oth"))
            grp = [r_act, r_mix, r_qT, r_kT, r_vext, r_oth]
            r_act.alias = tuple(grp[1:])
            for r in grp[1:]:
                r.alias = (r_act,)
            r_out = []
            r_cc = Res("cc")
            stat_i = [0]
            r_statg = [Res(f"stat{g}") for g in range(16)]

            def stat_col(n=1):
                g = stat_i[0] % 16
                stat_i[0] += 1
                return g * 8

            def act_op(out, in_, func, reads, writes, bias=None, scale=None, accum=None):
                kw = {}
                if bias is not None:
                    kw["bias"] = bias
                if scale is not None:
                    kw["scale"] = scale
                if accum is not None:
                    kw["accum_out"] = accum
                S.op("act", lambda e: e.activation(out=out, in_=in_, func=func, **kw), reads, writes)

            def dve(fn, reads, writes):
                S.op("dve", fn, reads, writes)

            def mm(out, pairs, reads, writes, first=True, last=True):
                def fn(e):
                    ins = None
                    n = len(pairs)
                    for i, (l, r) in enumerate(pairs):
                        ins = e.matmul(out, l, r, start=(first and i == 0), stop=(last and i == n - 1))
                    return ins
                S.op("pe", fn, reads, writes)

            def dma(eng, out, in_, reads, writes, sem, nc_ok=False):
                if nc_ok:
                    S.op(eng, lambda e: [e.dma_start(out=out, in_=in_, allow_slow_non_contiguous=True)],
                         reads, writes, dma=sem)
                else:
                    S.op(eng, lambda e: [e.dma_start(out=out, in_=in_)], reads, writes, dma=sem)

            def debug_dump(name, ap, res):
                if name in dbgd and not S.dry:
                    rl = list(res) if isinstance(res, (list, tuple)) else [res]
                    dma("pool", dbgd[name], ap, rl, [Res("dbgout")], "dbg")

            r_gath = [Res(f"gath{g}") for g in range(7)]
            r_cgin = [Res(f"cgin{g}") for g in range(7)]

            def issue_wload(blk, slot):
                k = ws.index(slot)
                g, r = divmod(BLK_ORDER.index(blk), 8)
                dma("sp", slot.t[:], cgout[g][r * 128:(r + 1) * 128, :], [r_gath[g]], [slot.res], f"ws{k}")

            def wget(blk):
                return WS.get(S, blk, issue_wload)

            def wview(slot, nk=16):
                return slot.t[:, 0:nk * 512].rearrange("p (kc n) -> p kc n", kc=nk)

            dma("sp", cv[:], cvd, [], [r_const], "misc")
            dma("sp", ident_f[:], identd, [], [r_const], "misc")
            dma("sp", maskT[:], maskTd, [], [r_const], "misc")
            dma("sp", rmask[:], rmaskd, [], [r_const], "misc")
            dma("sp", eye4[:], eye4d, [], [r_const], "misc")
            dma("sp", gb[:, 0:2], gbd, [], [r_const], "misc")
            dma("sp", cfg[:], cfgd, [], [r_cfg], "misc")
            dma("sp", brow[:, 0, :], bvod, [], [r_const, r_act], "misc")
            dma("sp", g3bc[:], g3d, [], [r_const], "misc")
            dma("pool", wg[:], wgd.rearrange("(kc p) n -> p kc n", p=128), [], [r_const], "W7", nc_ok=True)
            act_op(ident_b[:], ident_f[:], AF.Copy, [r_const], [r_const])
            dve(lambda e: e.memset(onesm[:], 1.0 / 128.0), [], [r_const])
            dve(lambda e: e.memset(ones4[:], 1.0), [], [r_const])
            dve(lambda e: e.memset(ones2b[:], 1.0), [], [r_const])
            dve(lambda e: e.memset(neghalf[:], -0.5), [], [r_const])
            dve(lambda e: e.tensor_scalar(cvx[:, 0:8], cv[:, BG:BG + 8], 0.5, None, ALU.mult), [r_const], [r_const])
            dve(lambda e: e.tensor_scalar(cvx[:, 8:16], cv[:, HG:HG + 8], 0.5, None, ALU.mult), [r_const], [r_const])
            dve(lambda e: e.tensor_scalar(gb[:, 2:3], gb[:, 1:2], -1.0, None, ALU.mult), [r_const], [r_const])
            dve(lambda e: e.tensor_copy(browb[:, 0, :], brow[:, 0, :]), [r_act], [r_act])
            dve(lambda e: e.tensor_copy(brow[:, 1, :], browb[:, 0, :]), [r_act], [r_act])
            dve(lambda e: e.tensor_tensor(brow[:, 2, :], brow[:, 0, :], brow[:, 1, :], ALU.subtract), [r_act], [r_act])
            dve(lambda e: e.tensor_copy(browb[:, 1, :], brow[:, 2, :]), [r_act], [r_act])
            dma("sp", bias_vo[0:1, :], browb[:, 0, :], [r_act], [r_const], "misc")
            dma("sp", bias_vo[1:2, :], browb[:, 1, :], [r_act], [r_const], "misc")
            dve(lambda e: e.memset(Cst[:], 0.0), [], r_C)
            dve(lambda e: e.memset(mcur[:], NEG), [], [r_mcur])
            dve(lambda e: e.memset(bsum[:], 0.0), [], [r_bsum])
            dve(lambda e: e.memset(qkhalo[:], 0.0), [], [r_qkh])
            dve(lambda e: e.memset(uhalo[:], 0.0), [], [r_uh])
            dve(lambda e: e.memset(ghalo[:], 0.0), [], [r_gh])

            for g in range(7):
                dma("pool", cgin[g][:, :], wsh[g], [], [r_cgin[g]], f"W{g}")

                def fg(e, g=g):
                    return [e.collective_compute("AllGather", ALU.bypass, replica_groups=[list(range(8))],
                                                 ins=[cgin[g][:, :]], outs=[cgout[g][:, :]])]
                S.op("pool", fg, [r_cgin[g]], [r_gath[g]], dma=f"G{g}", unit=1)

            def rs(c0):
                return r_statg[c0 // 8]

            def rstd_from_ss(ss_ap, n, out_ap, r_stat):
                def f1(e):
                    return e.tensor_scalar(out_ap, ss_ap, 1.0 / n, EPS, ALU.mult, ALU.add)
                dve(f1, [r_stat], [r_stat])
                S.op("pool", lambda e: e.tensor_tensor(out_ap, out_ap, neghalf[:, 0:1], ALU.pow),
                     [r_stat, r_const], [r_stat])

            def norm_T(src_buf, j, gcol):
                c0 = stat_col(2)
                r_stat = rs(c0)
                act_op(hn_tm[:], src_buf.t[:], AF.Square, [src_buf.res], [r_hn, r_stat], accum=stat[:, c0:c0 + 1])
                rstd_from_ss(stat[:, c0:c0 + 1], float(D), stat[:, c0 + 1:c0 + 2], r_stat)
                dve(lambda e: e.tensor_scalar(hn_tm[:], src_buf.t[:], stat[:, c0 + 1:c0 + 2], None, ALU.mult),
                    [src_buf.res, r_stat], [r_hn])
                for half in range(2):
                    ps = PS.get()
                    pb = ps.t[:].bitcast(BF16)

                    def ftr(e, pb=pb, half=half):
                        ins = None
                        for i in range(8):
                            kc = half * 8 + i
                            ins = e.transpose(pb[:, i * 128:(i + 1) * 128], hn_tm[:, kc * 128:(kc + 1) * 128], ident_b[:])
                        return ins
                    S.op("pe", ftr, [r_hn, r_const], [ps.res])
                    eng = "act" if half == 0 else "dve"

                    def fev(e, pb=pb, half=half, eng=eng):
                        ins = None
                        for i in range(8):
                            kc = half * 8 + i
                            o = hT[:, kc, j * 128:(j + 1) * 128]
                            if eng == "act":
                                ins = e.activation(out=o, in_=pb[:, i * 128:(i + 1) * 128], func=AF.Copy,
                                                   scale=cv[:, gcol + kc:gcol + kc + 1])
                            else:
                                ins = e.tensor_scalar(o, pb[:, i * 128:(i + 1) * 128], cv[:, gcol + kc:gcol + kc + 1],
                                                      None, ALU.mult)
                        return ins
                    S.op(eng, fev, [ps.res, r_const], [r_hT[j]])

            def load_x(j, sub):
                dma("sp", xm[j].t[:], xh[sub * 128:(sub + 1) * 128, :], [], [xm[j].res], f"x{j}")

            def fm_group(ps, wv, ci, ns):
                TT = 128 * ns
                mm(ps.t[:, 0:TT], [(wv[:, kc, ci * 128:(ci + 1) * 128], hT[:, kc, 0:TT]) for kc in range(KC)],
                   [r_hT[j] for j in range(ns)], [ps.res])

            def gates(ns, negcols):
                TT = 128 * ns
                ps_i, ps_f = PS.get(), PS.get()
                rh = [r_hT[j] for j in range(ns)]
                mm(ps_i.t[0:4, 0:TT], [(wg[:, kc, 0:4], hT[:, kc, 0:TT]) for kc in range(KC)], rh + [r_const], [ps_i.res])
                mm(ps_f.t[0:4, 0:TT], [(wg[:, kc, 4:8], hT[:, kc, 0:TT]) for kc in range(KC)], rh + [r_const], [ps_f.res])
                t_li, t_lf, t_bn, t_u, t_th = TMP.get(), TMP.get(), TMP.get(), TMP.get(), TMP.get()
                li, lf, bn, uu, th = (t.t[0:4, 0:TT] for t in (t_li, t_lf, t_bn, t_u, t_th))
                act_op(li, ps_i.t[0:4, 0:TT], AF.Identity, [ps_i.res, r_const], [t_li.res], bias=gb[:, 0:1])
                if negcols is not None:
                    t_ng = TMP.get()
                    dma("sp", t_ng.t[0:4, 0:TT], negmd[:, negcols:negcols + TT], [], [t_ng.res], "misc")
                    dve(lambda e: e.tensor_tensor(li, li, t_ng.t[0:4, 0:TT], ALU.add), [t_li.res, t_ng.res], [t_li.res])
                act_op(lf, ps_f.t[0:4, 0:TT], AF.Exp, [ps_f.res, r_const], [t_lf.res], bias=gb[:, 2:3], scale=-1.0)
                act_op(lf, lf, AF.Ln, [t_lf.res], [t_lf.res], bias=1.0)
                dve(lambda e: e.tensor_tensor_scan(bn, rmask[:, 0:TT], lf, 0.0, ALU.mult, ALU.add),
                    [t_lf.res, r_const], [t_bn.res])
                dve(lambda e: e.tensor_tensor(li, li, bn, ALU.add), [t_li.res, t_bn.res], [t_li.res])
                dve(lambda e: e.tensor_reduce(gsm[:, 0:ns], li.rearrange("p (c l) -> p c l", l=128), AX.X, ALU.max),
                    [t_li.res], [r_gsm])
                dve(lambda e: e.tensor_copy(gsm[:, 4:4 + ns], bn.rearrange("p (c l) -> p c l", l=128)[:, :, 127]),
                    [t_bn.res], [r_gsm])
                dve(lambda e: e.tensor_tensor_scan(gsm[:, 8:8 + ns], gsm[:, 0:ns], gsm[:, 4:4 + ns], mcur[:, 0:1],
                                                   ALU.max, ALU.subtract), [r_gsm, r_mcur], [r_gsm])
                dve(lambda e: e.tensor_tensor(gsm[:, 12:12 + ns], gsm[:, 8:8 + ns], gsm[:, 4:4 + ns], ALU.add),
                    [r_gsm], [r_gsm])
                dve(lambda e: e.tensor_copy(gsm[:, 16:17], mcur[:, 0:1]), [r_mcur, r_gsm], [r_gsm])
                if ns > 1:
                    dve(lambda e: e.tensor_copy(gsm[:, 17:16 + ns], gsm[:, 8:8 + ns - 1]), [r_gsm], [r_gsm])
                dve(lambda e: e.tensor_tensor(gsm[:, 20:20 + ns], gsm[:, 16:16 + ns], gsm[:, 12:12 + ns], ALU.subtract),
                    [r_gsm], [r_gsm])
                dve(lambda e: e.tensor_scalar(gsm[:, 20:20 + ns], gsm[:, 20:20 + ns], -100.0, None, ALU.max), [r_gsm], [r_gsm])
                act_op(gsm[:, 20:20 + ns], gsm[:, 20:20 + ns], AF.Exp, [r_gsm], [r_gsm])
                dve(lambda e: e.tensor_copy(mcur[:, 0:1], gsm[:, 8 + ns - 1:8 + ns]), [r_gsm], [r_mcur])
                dve(lambda e: e.tensor_reduce(gsm[:, 24:25], gsm[:, 4:4 + ns], AX.X, ALU.add), [r_gsm], [r_gsm])
                dve(lambda e: e.tensor_tensor(bsum[:, 0:1], bsum[:, 0:1], gsm[:, 24:25], ALU.add), [r_gsm, r_bsum], [r_bsum])
                mcb = gsm[:, 12:12 + ns].unsqueeze(2).to_broadcast([4, ns, 128])
                dve(lambda e: e.tensor_tensor(uu.rearrange("p (c l) -> p c l", l=128),
                                              li.rearrange("p (c l) -> p c l", l=128), mcb, ALU.subtract),
                    [t_li.res, r_gsm], [t_u.res])
                dve(lambda e: e.tensor_scalar(uu, uu, -100.0, -LN16, ALU.max, ALU.add), [t_u.res], [t_u.res])
                act_op(uu, uu, AF.Exp, [t_u.res], [t_u.res])
                dve(lambda e: e.tensor_tensor(th.rearrange("p (c l) -> p c l", l=128),
                                              bn.rearrange("p (c l) -> p c l", l=128), mcb, ALU.subtract),
                    [t_bn.res, r_gsm], [t_th.res])
                dve(lambda e: e.tensor_scalar(th, th, 80.0, None, ALU.min), [t_th.res], [t_th.res])
                act_op(th, th, AF.Exp, [t_th.res], [t_th.res])
                ps = PS.get()

                def ftr(e):
                    ins = None
                    for c in range(ns):
                        e.transpose(ps.t[:, c * 8:c * 8 + 4], uu[:, c * 128:(c + 1) * 128], ident_f[0:4, 0:4])
                        ins = e.transpose(ps.t[:, c * 8 + 4:c * 8 + 8], th[:, c * 128:(c + 1) * 128], ident_f[0:4, 0:4])
                    return ins
                S.op("pe", ftr, [t_u.res, t_th.res, r_const], [ps.res])
                dve(lambda e: e.tensor_copy(ut[:, 0:ns, :], ps.t[:, 0:8 * ns].rearrange("p (c k) -> p c k", k=8)),
                    [ps.res], [r_ut])
                t_dd = TMP.get()
                dd = t_dd.t[0:4, 0:4 * ns].rearrange("p (h c) -> p h c", h=4)
                dve(lambda e: e.tensor_tensor(dd, gsm[:, 20:20 + ns].unsqueeze(1).to_broadcast([4, 4, ns]),
                                              eye4[:].unsqueeze(2).to_broadcast([4, 4, ns]), ALU.mult),
                    [r_gsm, r_const], [t_dd.res])
                ps2 = PS.get()
                mm(ps2.t[:, 0:4 * ns], [(ones4[:], t_dd.t[0:4, 0:4 * ns])], [t_dd.res, r_const], [ps2.res])
                dve(lambda e: e.tensor_copy(dmb[:, :, 0:ns], ps2.t[:, 0:4 * ns].rearrange("p (h c) -> p h c", h=4)),
                    [ps2.res], [r_dmb])

            def qk_chunks(ns, slot, blk_first_chunk, maskcols, halo_only=False):
                TT = 128 * ns
                wv = wview(slot)
                for ci in range(4):
                    qc = blk_first_chunk + ci
                    ps = PS.get()
                    mm(ps.t[:, 0:TT], [(wv[:, kc, ci * 128:(ci + 1) * 128], hT[:, kc, 0:TT]) for kc in range(KC)],
                       [r_hT[j] for j in range(ns)] + [slot.res], [ps.res])
                    raw = QKR.get()
                    act_op(raw.t[:, 3:3 + TT], ps.t[:, 0:TT], AF.Identity, [ps.res, r_const], [raw.res],
                           bias=cv[:, BQK + qc:BQK + qc + 1])
                    if maskcols is not None:
                        tm = TMP.get()
                        dma("sp", tm.t[:, 0:TT], tokmaskd[:, maskcols:maskcols + TT], [], [tm.res], "misc")
                        dve(lambda e, raw=raw, tm=tm: e.tensor_tensor(raw.t[:, 3:3 + TT], raw.t[:, 3:3 + TT],
                                                                      tm.t[:, 0:TT], ALU.mult),
                            [raw.res, tm.res], [raw.res])
                    dve(lambda e, raw=raw, qc=qc: e.tensor_copy(raw.t[:, 0:3], qkhalo[:, qc, :]), [r_qkh], [raw.res])
                    dve(lambda e, raw=raw, qc=qc: e.tensor_copy(qkhalo[:, qc, :], raw.t[:, TT:TT + 3]), [raw.res], [r_qkh])
                    if halo_only:
                        continue
                    acc = TMP.get()
                    wc = QKW + qc * 4

                    dve(lambda e, raw=raw, acc=acc, wc=wc, qc=qc: e.tensor_scalar(
                        acc.t[:, 0:TT], raw.t[:, 0:TT], cv[:, wc:wc + 1], cv[:, QKB + qc:QKB + qc + 1], ALU.mult, ALU.add),
                        [raw.res, r_const], [acc.res])
                    for k in range(1, 4):
                        dve(lambda e, raw=raw, acc=acc, wc=wc, k=k: e.scalar_tensor_tensor(
                            acc.t[:, 0:TT], raw.t[:, k:k + TT], cv[:, wc + k:wc + k + 1], acc.t[:, 0:TT], ALU.mult, ALU.add),
                            [raw.res, acc.res, r_const], [acc.res])
                    if qc < 8:
                        act_op(qT_v[:, qc, 0:TT], acc.t[:, 0:TT], AF.Silu, [acc.res], [r_qT])
                    else:
                        act_op(kT_v[:, qc - 8, 0:TT], acc.t[:, 0:TT], AF.Silu, [acc.res], [r_kT])

            def tm_block(ns, slot, bcol, kind):
                wv = wview(slot)
                for j in range(ns):
                    ps = PS.get()
                    pairs = [(hT[:, kc, j * 128:(j + 1) * 128], wv[:, kc, :]) for kc in range(KC)]
                    pairs.append((ones2b[:], bias_vo[:, bcol:bcol + 512]))
                    mm(ps.t[:], pairs, [r_hT[j], slot.res, r_const], [ps.res])
                    if kind < 2:
                        o = vext_v[:, j, 2 * kind:2 * kind + 2, 0:256]
                        act_op(o, ps.t[:].rearrange("p (h e) -> p h e", h=2), AF.Copy, [ps.res], [r_vext])
                    else:
                        c0 = (kind - 2) * 512
                        act_op(oth_v[:, j, c0:c0 + 512], ps.t[:], AF.Tanh, [ps.res], [r_oth], scale=0.5)

            def kv_update(ns, j, h, kt, with_out, qcols):
                tok = slice(j * 128, (j + 1) * 128)
                vh = vext_v[:, j, h, :]
                if with_out:
                    ps_s = PS.get()
                    mm(ps_s.t[:, 0:128], [(kT_v[:, 2 * h + dc, tok], qT_v[:, 2 * h + dc, tok]) for dc in range(2)],
                       [r_kT, r_qT], [ps_s.res])
                    st = STL.get()
                    dve(lambda e: e.scalar_tensor_tensor(st.t[:], ps_s.t[:, 0:128], ut[:, j, h:h + 1], maskT[:],
                                                         ALU.mult, ALU.mult), [ps_s.res, r_ut, r_const], [st.res])
                    cb = CBF.get()
                    act_op(cb.t[:], Cst[:, h, :, :], AF.Copy, [r_C[h], r_dmb], [cb.res], scale=dmb[:, h, j:j + 1])
                    ps_p = PS.get()
                    mm(ps_p.t[:, 0:257], [(qT_v[:, 2 * h, tok], cb.t[:, 0, :]), (qT_v[:, 2 * h + 1, tok], cb.t[:, 1, :]),
                                          (st.t[:], vh)], [r_qT, cb.res, st.res, r_vext], [ps_p.res])
                for dc in range(2):
                    ps_k = PS.get()
                    mm(ps_k.t[:, 0:257], [(kt.t[:, h * 256 + dc * 128:h * 256 + (dc + 1) * 128], vh)],
                       [kt.res, r_vext], [ps_k.res])
                    dve(lambda e, ps_k=ps_k, dc=dc: e.scalar_tensor_tensor(
                        Cst[:, h, dc, :], Cst[:, h, dc, :], dmb[:, h, j:j + 1], ps_k.t[:, 0:257], ALU.mult, ALU.add),
                        [r_C[h], r_dmb, ps_k.res], [r_C[h]])
                if not with_out:
                    return None
                c0 = stat_col(6)
                r_stat = rs(c0)
                act_op(stat[:, c0:c0 + 1], ps_p.t[:, 256:257], AF.Abs, [ps_p.res], [r_stat])
                dve(lambda e: e.tensor_tensor(stat[:, c0:c0 + 1], stat[:, c0:c0 + 1], ut[:, j, 4 + h:5 + h], ALU.max),
                    [r_stat, r_ut], [r_stat])
                dve(lambda e: e.reciprocal(stat[:, c0:c0 + 1], stat[:, c0:c0 + 1]), [r_stat], [r_stat])
                hb = HH.get()
                act_op(hb.t[:], ps_p.t[:, 0:256], AF.Copy, [ps_p.res, r_stat], [hb.res], scale=stat[:, c0:c0 + 1])
                return hb, c0

            def head_norm(hb, c0, j, h, hm):
                r_stat = rs(c0)
                t6 = TMP.get()
                dve(lambda e: e.bn_stats(t6.t[:, 0:6], hb.t[:]), [hb.res], [t6.res])
                dve(lambda e: e.bn_aggr(stat[:, c0 + 1:c0 + 3], t6.t[:, 0:6]), [t6.res], [r_stat])
                dve(lambda e: e.tensor_scalar(stat[:, c0 + 3:c0 + 4], stat[:, c0 + 2:c0 + 3], EPS, None, ALU.add),
                    [r_stat], [r_stat])
                S.op("pool", lambda e: e.tensor_tensor(stat[:, c0 + 3:c0 + 4], stat[:, c0 + 3:c0 + 4], neghalf[:, 0:1],
                                                       ALU.pow), [r_stat, r_const], [r_stat])
                dve(lambda e: e.tensor_scalar(hb.t[:], hb.t[:], stat[:, c0 + 1:c0 + 2], stat[:, c0 + 3:c0 + 4],
                                              ALU.subtract, ALU.mult), [hb.res, r_stat], [hb.res])
                dve(lambda e: e.scalar_tensor_tensor(hm.t[:, h * 256:(h + 1) * 256], oth_v[:, j, h * 256:(h + 1) * 256],
                                                     1.0, hb.t[:], ALU.add, ALU.mult), [r_oth, hb.res], [hm.res])

            def k_token_major(j, with_halfscale=True):
                ps = PS.get()
                pb = ps.t[:].bitcast(BF16)

                def ftr(e):
                    ins = None
                    for c in range(8):
                        ins = e.transpose(pb[:, c * 128:(c + 1) * 128], kT_v[:, c, j * 128:(j + 1) * 128], ident_b[:])
                    return ins
                S.op("pe", ftr, [r_kT, r_const], [ps.res])
                kt = KTL.get()

                def fev(e):
                    ins = None
                    for h in range(4):
                        ins = e.tensor_scalar(kt.t[:, h * 256:(h + 1) * 256], pb[:, h * 256:(h + 1) * 256],
                                              ut[:, j, h:h + 1], None, ALU.mult)
                    return ins
                dve(fev, [ps.res, r_ut], [kt.res])
                return kt

            def set_vext_ones():
                dve(lambda e: e.memset(vext_v[:, :, :, 256:257], 1.0), [], [r_vext])

            load_x(0, 0)
            norm_T(xm[0], 0, G1)
            for kb in range(2):
                slot = wget(6 + kb)
                qk_chunks(1, slot, 8 + 4 * kb, 0, halo_only=True)
            dve(lambda e: e.tensor_copy(khalo0[:], qkhalo[:, 8:16, :]), [r_qkh], [r_kh0])
            for wi in range(NOWN):
                sub0 = 1 + 4 * wi
                for j in range(4):
                    load_x(j, sub0 + j)
                    norm_T(xm[j], j, G1)
                set_vext_ones()
                gates(4, 128 if wi == 0 else None)
                for kb in range(2):
                    slot = wget(6 + kb)
                    qk_chunks(4, slot, 8 + 4 * kb, 128 if wi == 0 else None)
                for vb in range(2):
                    slot = wget(8 + vb)
                    tm_block(4, slot, vb * 512, vb)
                if wi == 1:
                    debug_dump("z_hT", hT[:].rearrange("p k t -> p (k t)"), r_hT)
                    debug_dump("z_gsm", gsm[:, :], r_gsm)
                    debug_dump("z_ut", ut[:].rearrange("p c k -> p (c k)"), r_ut)
                    debug_dump("z_dmb", dmb[:].rearrange("p c k -> p (c k)"), r_dmb)
                    debug_dump("z_kT", kT_v.rearrange("p k t -> p (k t)"), r_kT)
                    debug_dump("z_vext", vext_v.rearrange("p j h e -> p (j h e)"), r_vext)
                for j in range(4):
                    kt = k_token_major(j)
                    for h in range(4):
                        kv_update(4, j, h, kt, False, None)
                debug_dump(f"z_C{wi}", Cst[:].rearrange("p h d e -> p (h d e)"), r_C)

            debug_dump("p0_C", Cst[:].rearrange("p h d e -> p (h d e)"), r_C)
            debug_dump("p0_m", mcur[:, 0:1], r_mcur)
            debug_dump("p0_b", bsum[:, 0:1], r_bsum)
            dma("pool", ccin[0:128, :], Cst[:].rearrange("p h d e -> p (h d e)"), r_C, [r_cc], "cc")
            dma("pool", ccin[128:129, 0:4].rearrange("o (h x) -> (o h) x", x=1), mcur[:, 0:1], [r_mcur], [r_cc], "cc")
            dma("pool", ccin[128:129, 4:8].rearrange("o (h x) -> (o h) x", x=1), bsum[:, 0:1], [r_bsum], [r_cc], "cc")
            r_ccout = Res("ccout")

            def fcc(e):
                return [e.collective_compute("AllGather", ALU.bypass, replica_groups=[list(range(8))],
                                             ins=[ccin[:, :]], outs=[ccout[:, :]])]
            S.op("pool", fcc, [r_cc], [r_ccout], dma="GX", unit=1)
            sc = ccout.rearrange("(r q) n -> q n r", q=129)
            dma("pool", gsm[:, 32:40], sc[128, 0:4, :], [r_ccout, r_gsm], [r_gsm], "cc", nc_ok=True)
            dma("pool", gsm[:, 40:48], sc[128, 4:8, :], [r_ccout, r_gsm], [r_gsm], "cc", nc_ok=True)
            t_d = TMP.get()
            d3 = t_d.t[0:4, 0:64].rearrange("p (j i) -> p j i", j=8)
            dve(lambda e: e.tensor_tensor(d3, gsm[:, 40:48].unsqueeze(1).to_broadcast([4, 8, 8]),
                                          cfg[:, 16:80].rearrange("p (j i) -> p j i", j=8), ALU.mult),
                [r_gsm, r_cfg], [t_d.res])
            dve(lambda e: e.tensor_reduce(gsm[:, 48:56], d3, AX.X, ALU.add), [t_d.res], [r_gsm])
            dve(lambda e: e.tensor_tensor(gsm[:, 48:56], gsm[:, 32:40], gsm[:, 48:56], ALU.subtract), [r_gsm], [r_gsm])
            dve(lambda e: e.tensor_tensor(gsm[:, 48:56], gsm[:, 48:56], cfg[:, 0:8], ALU.add), [r_gsm, r_cfg], [r_gsm])
            dve(lambda e: e.tensor_reduce(gsm[:, 56:57], gsm[:, 48:56], AX.X, ALU.max), [r_gsm], [r_gsm])
            dve(lambda e: e.tensor_scalar(gsm[:, 48:56], gsm[:, 48:56], gsm[:, 56:57], -100.0, ALU.subtract, ALU.max), [r_gsm], [r_gsm])
            act_op(gsm[:, 48:56], gsm[:, 48:56], AF.Exp, [r_gsm], [r_gsm])
            dve(lambda e: e.tensor_tensor(gsm[:, 48:56], gsm[:, 48:56], cfg[:, 8:16], ALU.mult), [r_gsm, r_cfg], [r_gsm])
            dve(lambda e: e.tensor_copy(mcur[:, 0:1], gsm[:, 56:57]), [r_gsm], [r_mcur])
            t_cd = TMP.get()
            cd = t_cd.t[0:4, 0:32].rearrange("p (h r) -> p h r", h=4)
            dve(lambda e: e.tensor_tensor(cd, gsm[:, 48:56].unsqueeze(1).to_broadcast([4, 4, 8]),
                                          eye4[:].unsqueeze(2).to_broadcast([4, 4, 8]), ALU.mult),
                [r_gsm, r_const], [t_cd.res])
            ps_c = PS.get()
            mm(ps_c.t[:, 0:32], [(ones4[:], t_cd.t[0:4, 0:32])], [t_cd.res, r_const], [ps_c.res])
            t_cb = TMP.get()
            dve(lambda e: e.tensor_copy(t_cb.t[:, 0:32], ps_c.t[:, 0:32]), [ps_c.res], [t_cb.res])
            dve(lambda e: e.memset(Cst[:], 0.0), [], r_C)
            for r in range(8):
                stg = cstage_v[r % 2]
                dma("pool", stg, ccout[r * 129:r * 129 + 128, :], [r_ccout], [r_act], f"cs{r % 2}")
                for h in range(4):
                    dve(lambda e, stg=stg, h=h, r=r: e.scalar_tensor_tensor(
                        Cst[:, h, :, :].rearrange("p d e -> p (d e)"), stg[:, h * 514:(h + 1) * 514],
                        t_cb.t[:, h * 8 + r:h * 8 + r + 1], Cst[:, h, :, :].rearrange("p d e -> p (d e)"),
                        ALU.mult, ALU.add), [r_act, t_cb.res, r_C[h]], [r_C[h]])
            debug_dump("ex_C", Cst[:].rearrange("p h d e -> p (h d e)"), r_C)
            debug_dump("ex_m", mcur[:, 0:1], r_mcur)
            debug_dump("ex_g", gsm[:, :], r_gsm)
            dve(lambda e: e.memset(qkhalo[:, 0:8, :], 0.0), [], [r_qkh])
            dve(lambda e: e.tensor_copy(qkhalo[:, 8:16, :], khalo0[:]), [r_kh0], [r_qkh])

            def mixer(ns, sub0, masked):
                TT = 128 * ns
                mcol = sub0 * 128 if masked else None
                dd = (sub0 == 3)
                for j in range(ns):
                    load_x(j, sub0 + j)
                    norm_T(xm[j], j, G1)
                if dd:
                    debug_dump("hT", hT[:].rearrange("p k t -> p (k t)"), r_hT)
                set_vext_ones()
                gates(ns, mcol)
                if dd:
                    debug_dump("gsm", gsm[:, :], r_gsm)
                    debug_dump("ut", ut[:].rearrange("p c k -> p (c k)"), r_ut)
                    debug_dump("dmb", dmb[:].rearrange("p c k -> p (c k)"), r_dmb)
                for b in range(4):
                    slot = wget(4 + b)
                    qk_chunks(ns, slot, 4 * b, mcol)
                for vb in range(4):
                    slot = wget(8 + vb)
                    tm_block(ns, slot, vb * 512, vb)
                if dd:
                    debug_dump("qT", qT_v.rearrange("p k t -> p (k t)"), r_qT)
                    debug_dump("kT", kT_v.rearrange("p k t -> p (k t)"), r_kT)
                    debug_dump("vext", vext_v.rearrange("p j h e -> p (j h e)"), r_vext)
                    debug_dump("oth", oth_v.rearrange("p j n -> p (j n)"), r_oth)
                for j in range(ns):
                    kt = k_token_major(j)
                    hm = HMT.get()
                    for h in range(4):
                        hb, c0 = kv_update(ns, j, h, kt, True, None)
                        head_norm(hb, c0, j, h, hm)
                    ps = PS.get()
                    pb = ps.t[:].bitcast(BF16)

                    def ftr(e, pb=pb, hm=hm):
                        ins = None
                        for c in range(8):
                            ins = e.transpose(pb[:, c * 128:(c + 1) * 128], hm.t[:, c * 128:(c + 1) * 128], ident_b[:])
                        return ins
                    S.op("pe", ftr, [hm.res, r_const], [ps.res])

                    def fev(e, pb=pb, j=j):
                        ins = None
                        for c in range(8):
                            ins = e.activation(out=mixT_v[:, 8 + c, j * 128:(j + 1) * 128], in_=pb[:, c * 128:(c + 1) * 128],
                                               func=AF.Copy, scale=cvx[:, 8 + c:9 + c])
                        return ins
                    S.op("act", fev, [ps.res, r_const], [r_mix])
                for half in range(2):
                    rh = [r_hT[j] for j in range(ns)]
                    sa = wget(0 + half)
                    wa = wview(sa)
                    ps_as = []
                    for ci in range(4):
                        ps_a = PS.get(pin=True)
                        mm(ps_a.t[:, 0:TT], [(wa[:, kc, ci * 128:(ci + 1) * 128], hT[:, kc, 0:TT]) for kc in range(KC)],
                           rh + [sa.res], [ps_a.res])
                        ps_as.append(ps_a)
                    sg = wget(2 + half)
                    wgv = wview(sg)
                    for ci in range(4):
                        c = half * 4 + ci
                        ps_a = ps_as[ci]
                        ps_g = PS.get()
                        mm(ps_g.t[:, 0:TT], [(wgv[:, kc, ci * 128:(ci + 1) * 128], hT[:, kc, 0:TT]) for kc in range(KC)],
                           rh + [sg.res], [ps_g.res])
                        tg = TMP.get()
                        act_op(tg.t[:, 0:TT], ps_g.t[:, 0:TT], AF.Tanh, [ps_g.res, r_const], [tg.res],
                               bias=cvx[:, c:c + 1], scale=0.5)
                        dve(lambda e, tg=tg: e.tensor_scalar(tg.t[:, 0:TT], tg.t[:, 0:TT], 0.5, 0.5, ALU.mult, ALU.add),
                            [tg.res], [tg.res])
                        up = UPR.get()
                        dve(lambda e, up=up, tg=tg, ps_a=ps_a, c=c: e.scalar_tensor_tensor(
                            up.t[:, 30:30 + TT], ps_a.t[:, 0:TT], cv[:, BA + c:BA + c + 1], tg.t[:, 0:TT], ALU.add, ALU.mult),
                            [ps_a.res, tg.res, r_const], [up.res])
                        PS.unpin(ps_a)
                        if mcol is not None:
                            tm = TMP.get()
                            dma("sp", tm.t[:, 0:TT], tokmaskd[:, mcol:mcol + TT], [], [tm.res], "misc")
                            dve(lambda e, up=up, tm=tm: e.tensor_tensor(up.t[:, 30:30 + TT], up.t[:, 30:30 + TT],
                                                                        tm.t[:, 0:TT], ALU.mult), [up.res, tm.res], [up.res])
                        dve(lambda e, up=up, c=c: e.tensor_copy(up.t[:, 0:30], uhalo[:, c, :]), [r_uh], [up.res])
                        dve(lambda e, up=up, c=c: e.tensor_copy(uhalo[:, c, :], up.t[:, TT:TT + 30]), [up.res], [r_uh])
                        xc = TMP.get()
                        wc = CW + c * 31
                        dve(lambda e, up=up, xc=xc, wc=wc, c=c: e.tensor_scalar(
                            xc.t[:, 0:TT], up.t[:, 0:TT], cv[:, wc:wc + 1], cv[:, CB + c:CB + c + 1], ALU.mult, ALU.add),
                            [up.res, r_const], [xc.res])
                        for k in range(1, 31):
                            dve(lambda e, up=up, xc=xc, wc=wc, k=k: e.scalar_tensor_tensor(
                                xc.t[:, 0:TT], up.t[:, k:k + TT], cv[:, wc + k:wc + k + 1], xc.t[:, 0:TT], ALU.mult, ALU.add),
                                [up.res, xc.res, r_const], [xc.res])
                        sq = TMP.get()
                        act_op(sq.t[:, 0:TT], xc.t[:, 0:TT], AF.Square, [xc.res], [sq.res])
                        nb = TT // 128
                        ps_st = PS.get()

                        def fst(e, xc=xc, sq=sq, ps_st=ps_st, nb=nb):
                            ins = None
                            for b in range(nb):
                                e.matmul(ps_st.t[:, b:b + 1], xc.t[:, b * 128:(b + 1) * 128], onesm[:, 0:1], start=True, stop=True)
                                ins = e.matmul(ps_st.t[:, 4 + b:5 + b], sq.t[:, b * 128:(b + 1) * 128], onesm[:, 0:1],
                                               start=True, stop=True)
                            return ins
                        S.op("pe", fst, [xc.res, sq.res, r_const], [ps_st.res])
                        c1 = stat_col(8)
                        c2 = stat_col(8)
                        rs1, rs2 = rs(c1), rs(c2)
                        dve(lambda e, c1=c1, ps_st=ps_st: e.tensor_copy(stat[:, c1:c1 + 8], ps_st.t[:, 0:8]), [ps_st.res], [rs1])
                        dve(lambda e, c1=c1, c2=c2, nb=nb: e.tensor_tensor(stat[:, c2:c2 + nb], stat[:, c1:c1 + nb],
                                                                        stat[:, c1:c1 + nb], ALU.mult), [rs1], [rs2])
                        dve(lambda e, c1=c1, c2=c2, nb=nb: e.scalar_tensor_tensor(
                            stat[:, c2:c2 + nb], stat[:, c1 + 4:c1 + 4 + nb], EPS, stat[:, c2:c2 + nb], ALU.add, ALU.subtract),
                            [rs1, rs2], [rs2])
                        S.op("pool", lambda e, c2=c2, nb=nb: e.tensor_tensor(stat[:, c2:c2 + nb], stat[:, c2:c2 + nb],
                                                                           neghalf[:, 0:nb], ALU.pow), [rs2, r_const], [rs2])
                        dve(lambda e, c1=c1, c2=c2, nb=nb: e.scalar_tensor_tensor(
                            stat[:, c2 + 4:c2 + 4 + nb], stat[:, c1:c1 + nb], -1.0, stat[:, c2:c2 + nb], ALU.mult, ALU.mult),
                            [rs1, rs2], [rs2])
                        dve(lambda e, c2=c2: e.tensor_scalar(stat[:, c2:c2 + 8], stat[:, c2:c2 + 8], 128.0, None, ALU.mult),
                            [rs2], [rs2])
                        dt1, dt2 = TMP.get(), TMP.get()
                        for b in range(nb):
                            dve(lambda e, dt1=dt1, c2=c2, b=b: e.tensor_scalar(
                                dt1.t[:, b * 128:(b + 1) * 128], ident_f[:], stat[:, c2 + b:c2 + b + 1], None, ALU.mult),
                                [rs2, r_const], [dt1.res])
                            dve(lambda e, dt2=dt2, c2=c2, b=b: e.tensor_scalar(
                                dt2.t[:, b * 128:(b + 1) * 128], ident_f[:], stat[:, c2 + 4 + b:c2 + 5 + b], None, ALU.mult),
                                [rs2, r_const], [dt2.res])
                        ps_r, ps_n = PS.get(), PS.get()

                        def fbc(e, dt1=dt1, dt2=dt2, ps_r=ps_r, ps_n=ps_n, nb=nb):
                            ins = None
                            for b in range(nb):
                                e.matmul(ps_r.t[:, b * 128:(b + 1) * 128], onesm[:], dt1.t[:, b * 128:(b + 1) * 128],
                                         start=True, stop=True)
                                ins = e.matmul(ps_n.t[:, b * 128:(b + 1) * 128], onesm[:], dt2.t[:, b * 128:(b + 1) * 128],
                                               start=True, stop=True)
                            return ins
                        S.op("pe", fbc, [dt1.res, dt2.res, r_const], [ps_r.res, ps_n.res])
                        dve(lambda e, xc=xc, ps_r=ps_r: e.tensor_tensor(xc.t[:, 0:TT], xc.t[:, 0:TT], ps_r.t[:, 0:TT], ALU.mult),
                            [xc.res, ps_r.res], [xc.res])
                        dve(lambda e, xc=xc, ps_n=ps_n: e.tensor_tensor(xc.t[:, 0:TT], xc.t[:, 0:TT], ps_n.t[:, 0:TT], ALU.add),
                            [xc.res, ps_n.res], [xc.res])
                        act_op(mixT_v[:, c, 0:TT], xc.t[:, 0:TT], AF.Silu, [xc.res, r_const], [r_mix],
                               bias=cv[:, GNB + c:GNB + c + 1], scale=cv[:, GNG + c:GNG + c + 1])
                if dd:
                    debug_dump("mixT", mixT_v.rearrange("p k t -> p (k t)"), r_mix)
                    debug_dump("C1", Cst[:].rearrange("p h d e -> p (h d e)"), r_C)
                for cb in range(4):
                    slot = wget(12 + cb)
                    wv = wview(slot)
                    for j in range(ns):
                        ps = PS.get()
                        mm(ps.t[:], [(mixT_v[:, kc, j * 128:(j + 1) * 128], wv[:, kc, :]) for kc in range(KC)],
                           [r_mix, slot.res], [ps.res])
                        dve(lambda e, ps=ps, j=j, cb=cb: e.tensor_tensor(
                            xm[j].t[:, cb * 512:(cb + 1) * 512], xm[j].t[:, cb * 512:(cb + 1) * 512], ps.t[:], ALU.add),
                            [ps.res, xm[j].res], [xm[j].res])

            def ffn(ns, halo_only, out_row0, gmaskcols):
                TT = 128 * ns
                if not halo_only and out_row0 == 0:
                    debug_dump("xmid", xm[0].t[:], xm[0].res)
                    debug_dump("ghalo", ghalo[:].rearrange("p c k -> p (c k)"), r_gh)
                for j in range(ns):
                    norm_T(xm[j], j, G2)
                rh = [r_hT[j] for j in range(ns)]
                for blk in range(11):
                    sg = wget(16 + blk)
                    wgt_v = wview(sg)
                    accl = []
                    for ci in range(4):
                        jj = blk * 4 + ci
                        ps_g = PS.get()
                        mm(ps_g.t[:, 0:TT], [(wgt_v[:, kc, ci * 128:(ci + 1) * 128], hT[:, kc, 0:TT]) for kc in range(KC)],
                           rh + [sg.res], [ps_g.res])
                        raw = QKR.get()
                        act_op(raw.t[:, 2:2 + TT], ps_g.t[:, 0:TT], AF.Copy, [ps_g.res], [raw.res])
                        dve(lambda e, raw=raw, jj=jj: e.tensor_copy(raw.t[:, 0:2], ghalo[:, jj, :]), [r_gh], [raw.res])
                        dve(lambda e, raw=raw, jj=jj: e.tensor_copy(ghalo[:, jj, :], raw.t[:, TT:TT + 2]), [raw.res], [r_gh])
                        if halo_only:
                            continue
                        acc = TMP.get()
                        accl.append(acc)
                        wc = FW + jj * 3
                        dve(lambda e, raw=raw, acc=acc, wc=wc, jj=jj: e.tensor_scalar(
                            acc.t[:, 0:TT], raw.t[:, 0:TT], cv[:, wc:wc + 1], cv[:, FB + jj:FB + jj + 1], ALU.mult, ALU.add),
                            [raw.res, r_const], [acc.res])
                        for k in range(1, 3):
                            dve(lambda e, raw=raw, acc=acc, wc=wc, k=k: e.scalar_tensor_tensor(
                                acc.t[:, 0:TT], raw.t[:, k:k + TT], cv[:, wc + k:wc + k + 1], acc.t[:, 0:TT], ALU.mult, ALU.add),
                                [raw.res, acc.res, r_const], [acc.res])
                        act_op(acc.t[:, 0:TT], acc.t[:, 0:TT], AF.Silu, [acc.res], [acc.res])
                    if halo_only:
                        continue
                    sv = wget(27 + blk)
                    wvl_v = wview(sv)
                    for ci in range(4):
                        jj = blk * 4 + ci
                        acc = accl[ci]
                        ps_v = PS.get()
                        mm(ps_v.t[:, 0:TT], [(wvl_v[:, kc, ci * 128:(ci + 1) * 128], hT[:, kc, 0:TT]) for kc in range(KC)],
                           rh + [sv.res], [ps_v.res])
                        dve(lambda e, acc=acc, ps_v=ps_v, jj=jj: e.tensor_tensor(act_v[:, jj, 0:TT], acc.t[:, 0:TT],
                                                                                 ps_v.t[:, 0:TT], ALU.mult),
                            [acc.res, ps_v.res], [r_act])
                if halo_only:
                    if gmaskcols is not None:
                        tm = TMP.get()
                        dma("sp", tm.t[:, 0:2], tokmaskd[:, gmaskcols:gmaskcols + 2], [], [tm.res], "misc")
                        dve(lambda e: e.tensor_tensor(ghalo[:], ghalo[:], tm.t[:, 0:2].unsqueeze(1).to_broadcast([128, NFC, 2]),
                                                      ALU.mult), [tm.res, r_gh], [r_gh])
                    return
                if out_row0 == 0:
                    debug_dump("act", act_v.rearrange("p k t -> p (k t)"), r_act)
                for cb in range(4):
                    accs = [PS.get() for _ in range(ns)]
                    for q in range(4):
                        slot = wget(38 + cb * 4 + q)
                        wv = wview(slot, 11)
                        for j in range(ns):
                            mm(accs[j].t[:], [(act_v[:, q * 11 + i, j * 128:(j + 1) * 128], wv[:, i, :]) for i in range(11)],
                               [r_act, slot.res], [accs[j].res], first=(q == 0), last=(q == 3))
                    for j in range(ns):
                        dve(lambda e, j=j, cb=cb, a=accs[j]: e.tensor_tensor(
                            xm[j].t[:, cb * 512:(cb + 1) * 512], xm[j].t[:, cb * 512:(cb + 1) * 512], a.t[:], ALU.add),
                            [accs[j].res, xm[j].res], [xm[j].res])
                for j in range(ns):
                    c0 = stat_col(2)
                    r_stat = rs(c0)
                    act_op(hn_tm[:], xm[j].t[:], AF.Square, [xm[j].res], [r_hn, r_stat], accum=stat[:, c0:c0 + 1])
                    rstd_from_ss(stat[:, c0:c0 + 1], float(D), stat[:, c0 + 1:c0 + 2], r_stat)
                    dve(lambda e, j=j, c0=c0: e.scalar_tensor_tensor(xm[j].t[:], xm[j].t[:], stat[:, c0 + 1:c0 + 2], g3bc[:],
                                                                     ALU.mult, ALU.mult), [xm[j].res, r_stat, r_const], [xm[j].res])
                    ro = Res("out")
                    r_out.append(ro)
                    dma("sp", outd[out_row0 + j * 128:out_row0 + (j + 1) * 128, :], xm[j].t[:], [xm[j].res], [ro], f"x{j}")

            mixer(2, 1, True)
            ffn(2, True, None, 382)
            for st in range(NOWN):
                mixer(4, 3 + 4 * st, False)
                ffn(4, False, st * 512, None)
            S.op("sp", lambda e: None, r_out, [])

        S0 = Sched()
        S0.dry = True
        emit(S0)
        WS.pos = 0
        S1 = Sched()
        emit(S1)
        S1.prune()

        with nc.Block() as block:
            @block.tensor
            def _(e):
                S1.replay("pe", e, sems)

            @block.scalar
            def _(e):
                S1.replay("act", e, sems)

            @block.vector
            def _(e):
                S1.replay("dve", e, sems)

            @block.gpsimd
            def _(e):
                S1.replay("pool", e, sems)

            @block.sync
            def _(e):
                S1.replay("sp", e, sems)
    return nc


def _colvec(a, nchunk):
    return np.ascontiguousarray(np.asarray(a, np.float32).reshape(nchunk, 128).T)


def _prep(inputs, NOWN):
    f = lambda k: np.asarray(inputs[k], np.float32)
    x = f("x")
    TOK = 512 * NOWN
    b_in = f("b_in")
    cv = np.zeros((128, NCV), np.float32)
    cv[:, G1:G1 + 16] = _colvec(f("norm_mix_g"), 16)
    cv[:, G2:G2 + 16] = _colvec(f("norm_ffn_g"), 16)
    cv[:, BA:BA + 8] = _colvec(b_in[0:1024], 8)
    cv[:, BG:BG + 8] = _colvec(b_in[1024:2048], 8)
    cv[:, BQK:BQK + 16] = _colvec(b_in[2048:4096], 16)
    cv[:, CW:CW + 248] = f("conv_dw_w").reshape(31, 8, 128).transpose(2, 1, 0).reshape(128, 248)
    cv[:, CB:CB + 8] = _colvec(f("conv_dw_b"), 8)
    cv[:, GNG:GNG + 8] = _colvec(f("conv_gn_g"), 8)
    cv[:, GNB:GNB + 8] = _colvec(f("conv_gn_b"), 8)
    cv[:, QKW:QKW + 64] = f("qk_conv_w").reshape(4, 16, 128).transpose(2, 1, 0).reshape(128, 64)
    cv[:, QKB:QKB + 16] = _colvec(f("qk_conv_b"), 16)
    cv[:, FW:FW + 132] = f("ffn_conv_w").reshape(3, NFC, 128).transpose(2, 1, 0).reshape(128, 132)
    cv[:, FB:FB + NFC] = _colvec(f("ffn_conv_b"), NFC)
    cv[:, HG:HG + 8] = _colvec(f("mlstm_hn_g"), 8)
    w_in, w_out, w_up, w_down = f("w_in"), f("w_out"), f("w_up"), f("w_down")

    def blk_host(i):
        if i < 12:
            m, nk = w_in[:, i * 512:(i + 1) * 512], 16
        elif i < 16:
            m, nk = w_out[:, (i - 12) * 512:(i - 11) * 512], 16
        elif i < 38:
            m, nk = w_up[:, (i - 16) * 512:(i - 15) * 512], 16
        else:
            cb, q = divmod(i - 38, 4)
            m, nk = w_down[q * 1408:(q + 1) * 1408, cb * 512:(cb + 1) * 512], 11
        o = np.zeros((128, 8192), np.float32)
        o[:, 0:nk * 512] = m.reshape(nk, 128, 512).transpose(1, 0, 2).reshape(128, nk * 512)
        return o
    common = {
        "cv": cv,
        "g3bc": np.ascontiguousarray(np.broadcast_to(f("norm_final_g")[None, :], (128, D))),
        "bvo": np.ascontiguousarray(b_in[4096:6144][None, :]),
        "gbias": np.ascontiguousarray(np.stack([b_in[6144:6148], b_in[6148:6152]], axis=1)),
        "wgate": np.ascontiguousarray(w_in[:, 6144:6152]),
        "ident": np.eye(128, dtype=np.float32),
        "maskT": np.triu(np.ones((128, 128), np.float32)),
        "rmask": np.ascontiguousarray(np.broadcast_to((np.arange(512) % 128 != 0).astype(np.float32)[None, :], (4, 512))),
        "eye4": np.eye(4, dtype=np.float32),
    }
    maps = []
    for c in range(8):
        b, seg = divmod(c, 4)
        t0 = seg * TOK
        lo = t0 - 384
        xh = np.zeros((TOK + 384, D), np.float32)
        s = max(lo, 0)
        xh[s - lo:] = x[b, s:t0 + TOK]
        valid = ((lo + np.arange(640)) >= 0)
        tokmask = np.ascontiguousarray(np.broadcast_to(valid.astype(np.float32)[None, :], (128, 640)))
        negm = np.ascontiguousarray(np.broadcast_to(np.where(valid, 0.0, NEG).astype(np.float32)[None, :], (4, 640)))
        cfg = np.zeros((80,), np.float32)
        for j in range(8):
            ok = (j // 4 == b) and (j % 4 < seg)
            cfg[j] = 0.0 if ok else NEG
            cfg[8 + j] = 1.0 if ok else 0.0
            if ok:
                for i in range(8):
                    if i // 4 == b and j < i < c:
                        cfg[16 + j * 8 + i] = 1.0
        wsh = np.zeros((7, 128, 8192), np.float32)
        for g in range(7):
            if g * 8 + c < len(BLK_ORDER):
                wsh[g] = blk_host(BLK_ORDER[g * 8 + c])
        m = dict(common)
        m.update({"xh": xh, "tokmask": tokmask, "negm": negm, "wsh": wsh,
                  "cfg": np.ascontiguousarray(np.broadcast_to(cfg[None, :], (4, 80)))})
        maps.append(m)
    return maps


_NC_CACHE = {}


def _run(inputs, NOWN, dbg=None):
    maps = _prep(inputs, NOWN)
    key = (NOWN, tuple(n for n, _ in (dbg or [])))
    if key not in _NC_CACHE:
        _NC_CACHE[key] = build(NOWN, dbg)
    nc = _NC_CACHE[key]
    res = run_bass_kernel_spmd(nc, maps, core_ids=list(range(8)))
    TOK = 512 * NOWN
    out = np.zeros((2, 4 * TOK, D), np.float32)
    for c in range(8):
        b, seg = divmod(c, 4)
        out[b, seg * TOK:(seg + 1) * TOK] = res.results[c]["out"]
    if dbg:
        return out, res.results
    return out


def kernel(**inputs):
    return _run(inputs, 8)
```
